# Optimizing a Trainium2 kernel written in Bass

```python
import jax, jax.numpy as jnp
from jax import lax
import numpy as np

D_MODEL = 1024
BATCH = 32
SEQ = 256
DEPTH = 1
DEC_BATCH = 4
DEC_SEQ = 2048
PAST_LEN = 256

GRID_W = 64
EPS = 1e-6
D_SSD = 2 * D_MODEL
SSD_HEADDIM = 64
SSD_HEADS = D_SSD // SSD_HEADDIM
SSD_STATE = 128
SSD_GROUPS = 4
SSD_CONV = 3
SSD_CHUNK = 128
BC_W = SSD_GROUPS * SSD_STATE
CONV_CH = D_SSD + 2 * BC_W
D_SGU = D_MODEL
SGU_CHUNK = 128
SGU_GROUP_DIM = 128
SGU_GROUPS = D_SGU // SGU_GROUP_DIM
D_FF = 2816
FFN_CONV = 3
SPLIT_Z = D_SSD
SPLIT_XBC = SPLIT_Z + CONV_CH
SPLIT_DT = SPLIT_XBC + 2 * SSD_HEADS
SPLIT_U = SPLIT_DT + D_SGU
SPLIT_V = SPLIT_U + D_SGU
SPLIT_GA = SPLIT_V + D_MODEL
IN_COLS = SPLIT_GA + D_MODEL

kernel_name = "hybrid_ssd_sgu_convffn_flow_step"


def _rmsnorm(x, w):
    xf = x.astype(jnp.float32)
    y = xf * lax.rsqrt(jnp.mean(xf * xf, axis=-1, keepdims=True) + EPS)
    return (y * w.astype(jnp.float32)).astype(x.dtype)


def _layernorm(x, w, b):
    xf = x.astype(jnp.float32)
    mu = jnp.mean(xf, axis=-1, keepdims=True)
    var = jnp.mean(jnp.square(xf - mu), axis=-1, keepdims=True)
    y = (xf - mu) * lax.rsqrt(var + EPS)
    return (y * w.astype(jnp.float32) + b.astype(jnp.float32)).astype(x.dtype)


def _conv1d_centred(x, w, b):
    k = w.shape[0]
    y = lax.conv_general_dilated(x, w[:, None, :].astype(x.dtype), window_strides=(1,),
                                 padding=((k // 2, k // 2),), dimension_numbers=("NWC", "WIO", "NWC"),
                                 feature_group_count=x.shape[-1])
    return y + b


def _conv2d_grid(x, w, b):
    bsz, l, ch = x.shape
    rows = l // GRID_W
    xg = x.reshape(bsz, rows, GRID_W, ch)
    y = lax.conv_general_dilated(xg, w[:, :, None, :].astype(x.dtype), window_strides=(1, 1),
                                 padding=((1, 1), (1, 1)), dimension_numbers=("NHWC", "HWIO", "NHWC"),
                                 feature_group_count=ch)
    return y.reshape(bsz, l, ch) + b


def _ssd_chunked(xdt, a, B, C, h0):
    bsz, l, nh, p = xdt.shape
    g, n = B.shape[2], B.shape[3]
    r = nh // g
    q = SSD_CHUNK
    nc = l // q
    x = xdt.astype(jnp.float32).reshape(bsz, nc, q, g, r, p)
    Bc = B.astype(jnp.float32).reshape(bsz, nc, q, g, n)
    Cc = C.astype(jnp.float32).reshape(bsz, nc, q, g, n)
    a_cs = jnp.cumsum(jnp.transpose(a.astype(jnp.float32).reshape(bsz, nc, q, g, r), (0, 1, 3, 4, 2)), axis=-1)
    diff = a_cs[..., :, None] - a_cs[..., None, :]
    mask = jnp.tril(jnp.ones((q, q), dtype=bool))
    L = jnp.exp(jnp.where(mask, diff, -jnp.inf))
    CB = jnp.einsum("bcign,bcjgn->bcgij", Cc, Bc)
    y_diag = jnp.einsum("bcgrij,bcjgrp->bcigrp", CB[:, :, :, None] * L, x)
    decay_states = jnp.exp(a_cs[..., -1:] - a_cs)
    states = jnp.einsum("bcjgn,bcgrj,bcjgrp->bcgrpn", Bc, decay_states, x)
    chunk_decay = jnp.exp(a_cs[..., -1])

    def step(h, inp):
        dec, st = inp
        return dec[..., None, None] * h + st, h

    h_init = h0.astype(jnp.float32).reshape(bsz, g, r, p, n)
    h_final, h_prev = lax.scan(step, h_init, (jnp.moveaxis(chunk_decay, 1, 0), jnp.moveaxis(states, 1, 0)))
    h_prev = jnp.moveaxis(h_prev, 0, 1)
    y_off = jnp.einsum("bcign,bcgri,bcgrpn->bcigrp", Cc, jnp.exp(a_cs), h_prev)
    y = (y_diag + y_off).reshape(bsz, l, nh, p)
    return y, h_final.reshape(bsz, nh, p, n)


def _token_mixer(h, h0_f, h0_b, lp):
    bsz, l, _ = h.shape
    proj = h @ lp["w_in"]
    z, xbc, dt_raw, u, v, ga, gb = jnp.split(proj, [SPLIT_Z, SPLIT_XBC, SPLIT_DT, SPLIT_U, SPLIT_V, SPLIT_GA], axis=-1)

    xbc = jax.nn.silu(_conv1d_centred(xbc, lp["ssd_conv_w"], lp["ssd_conv_b"]))
    xs, Bm, Cm = jnp.split(xbc, [D_SSD, D_SSD + BC_W], axis=-1)
    xs = xs.reshape(bsz, l, SSD_HEADS, SSD_HEADDIM).astype(jnp.float32)
    Bm = Bm.reshape(bsz, l, SSD_GROUPS, SSD_STATE)
    Cm = Cm.reshape(bsz, l, SSD_GROUPS, SSD_STATE)
    dt = jax.nn.softplus(dt_raw.reshape(bsz, l, 2, SSD_HEADS).astype(jnp.float32)
                         + lp["ssd_dt_bias"].astype(jnp.float32))
    A = -jnp.exp(lp["ssd_a_log"].astype(jnp.float32))
    y_f, hf = _ssd_chunked(xs * dt[:, :, 0, :, None], dt[:, :, 0] * A[0], Bm, Cm, h0_f)
    flip = lambda t: t[:, ::-1]
    y_b, hb = _ssd_chunked(flip(xs * dt[:, :, 1, :, None]), flip(dt[:, :, 1] * A[1]), flip(Bm), flip(Cm), h0_b)
    y = y_f + flip(y_b) + lp["ssd_d"].astype(jnp.float32)[:, None] * xs
    y = y.reshape(bsz, l, D_SSD) * jax.nn.silu(z.astype(jnp.float32))
    y_ssd = _rmsnorm(y, lp["ssd_norm"]).astype(h.dtype)

    vn = _layernorm(v, lp["sgu_norm_w"], lp["sgu_norm_b"])
    vc = vn.reshape(bsz, l // SGU_CHUNK, SGU_CHUNK, SGU_GROUPS, SGU_GROUP_DIM)
    s = jnp.einsum("gij,bcjgd->bcigd", lp["sgu_w"], vc) + jnp.transpose(lp["sgu_b"])[:, :, None]
    y_sgu = u * s.reshape(bsz, l, D_SGU)

    merged = jax.nn.sigmoid(ga) * (y_ssd @ lp["w_branch_ssd"]) + jax.nn.sigmoid(gb) * (y_sgu @ lp["w_branch_sgu"])
    return merged @ lp["w_out"], hf, hb


def _conv_ffn(h, on_grid, lp):
    up = h @ lp["ffn_w_up"]
    if on_grid:
        up = _conv2d_grid(up, lp["ffn_conv_w"], lp["ffn_conv_b"])
    else:
        up = _conv1d_centred(up, lp["ffn_conv_w"][FFN_CONV // 2], lp["ffn_conv_b"])
    a, val = jnp.split(up, 2, axis=-1)
    return (jax.nn.gelu(a) * val) @ lp["ffn_w_down"]


def _trunk_layer(x, mod, h0_f, h0_b, on_grid, lp):
    shift1, scale1, gate1, shift2, scale2, gate2 = jnp.split(mod[:, None, :], 6, axis=-1)
    h = _rmsnorm(x, lp["norm_mix_pre"]) * (1 + scale1) + shift1
    mix, hf, hb = _token_mixer(h, h0_f, h0_b, lp)
    x = x + gate1 * _rmsnorm(mix, lp["norm_mix_post"])
    h = _rmsnorm(x, lp["norm_ffn_pre"]) * (1 + scale2) + shift2
    x = x + gate2 * _rmsnorm(_conv_ffn(h, on_grid, lp), lp["norm_ffn_post"])
    return x, hf, hb


def setup_inputs(seed: int = 0) -> dict:
    key = jax.random.key(seed)
    ks = iter(jax.random.split(key, 40))
    nrm = lambda shape, s: jax.random.normal(next(ks), shape, jnp.float32) * s
    gain = lambda shape: 1.0 + nrm(shape, 0.02)
    dt0 = jnp.exp(jax.random.uniform(next(ks), (DEPTH, 2, SSD_HEADS), jnp.float32,
                                     np.log(1e-3).astype(np.float32), np.log(1e-1).astype(np.float32)))
    st_shape = (DEC_BATCH, DEPTH, SSD_HEADS, SSD_HEADDIM, SSD_STATE)
    return {
        "x_prompt": nrm((BATCH, SEQ, D_MODEL), 1.0),
        "x_sample": nrm((DEC_BATCH, DEC_SEQ, D_MODEL), 1.0),
        "state_ssd_fwd": nrm(st_shape, 0.5),
        "state_ssd_bwd": nrm(st_shape, 0.5),
        "c": nrm((DEC_BATCH, D_MODEL), 1.0),
        "c_ctx": nrm((D_MODEL,), 1.0),
        "w_mod": nrm((DEPTH, D_MODEL, 6 * D_MODEL), D_MODEL ** -0.5),
        "b_mod": nrm((DEPTH, 6 * D_MODEL), 0.02),
        "norm_mix_pre": gain((DEPTH, D_MODEL)),
        "norm_mix_post": gain((DEPTH, D_MODEL)),
        "norm_ffn_pre": gain((DEPTH, D_MODEL)),
        "norm_ffn_post": gain((DEPTH, D_MODEL)),
        "w_in": nrm((DEPTH, D_MODEL, IN_COLS), D_MODEL ** -0.5),
        "ssd_conv_w": nrm((DEPTH, SSD_CONV, CONV_CH), SSD_CONV ** -0.5),
        "ssd_conv_b": nrm((DEPTH, CONV_CH), 0.02),
        "ssd_a_log": jnp.log(jax.random.uniform(next(ks), (DEPTH, 2, SSD_HEADS), jnp.float32, 1.0, 16.0)),
        "ssd_dt_bias": dt0 + jnp.log(-jnp.expm1(-dt0)),
        "ssd_d": gain((DEPTH, SSD_HEADS)),
        "ssd_norm": gain((DEPTH, D_SSD)),
        "sgu_norm_w": gain((DEPTH, D_SGU)),
        "sgu_norm_b": nrm((DEPTH, D_SGU), 0.02),
        "sgu_w": nrm((DEPTH, SGU_GROUPS, SGU_CHUNK, SGU_CHUNK), SGU_CHUNK ** -0.5),
        "sgu_b": gain((DEPTH, SGU_GROUPS, SGU_CHUNK)),
        "w_branch_ssd": nrm((DEPTH, D_SSD, D_MODEL), D_SSD ** -0.5),
        "w_branch_sgu": nrm((DEPTH, D_SGU, D_MODEL), D_SGU ** -0.5),
        "w_out": nrm((DEPTH, D_MODEL, D_MODEL), D_MODEL ** -0.5),
        "ffn_w_up": nrm((DEPTH, D_MODEL, 2 * D_FF), D_MODEL ** -0.5),
        "ffn_conv_w": nrm((DEPTH, FFN_CONV, FFN_CONV, 2 * D_FF), 1.0 / FFN_CONV),
        "ffn_conv_b": nrm((DEPTH, 2 * D_FF), 0.02),
        "ffn_w_down": nrm((DEPTH, D_FF, D_MODEL), D_FF ** -0.5),
    }


def reference(x_prompt, x_sample, state_ssd_fwd, state_ssd_bwd, c, c_ctx, w_mod, b_mod,
              norm_mix_pre, norm_mix_post, norm_ffn_pre, norm_ffn_post, w_in, ssd_conv_w, ssd_conv_b,
              ssd_a_log, ssd_dt_bias, ssd_d, ssd_norm, sgu_norm_w, sgu_norm_b, sgu_w, sgu_b,
              w_branch_ssd, w_branch_sgu, w_out, ffn_w_up, ffn_conv_w, ffn_conv_b, ffn_w_down):
    xp, xs = x_prompt, x_sample
    new_f, new_b = [], []
    for i in range(DEPTH):
        lp = {
            "norm_mix_pre": norm_mix_pre[i], "norm_mix_post": norm_mix_post[i],
            "norm_ffn_pre": norm_ffn_pre[i], "norm_ffn_post": norm_ffn_post[i],
            "w_in": w_in[i], "ssd_conv_w": ssd_conv_w[i], "ssd_conv_b": ssd_conv_b[i],
            "ssd_a_log": ssd_a_log[i], "ssd_dt_bias": ssd_dt_bias[i], "ssd_d": ssd_d[i], "ssd_norm": ssd_norm[i],
            "sgu_norm_w": sgu_norm_w[i], "sgu_norm_b": sgu_norm_b[i], "sgu_w": sgu_w[i], "sgu_b": sgu_b[i],
            "w_branch_ssd": w_branch_ssd[i], "w_branch_sgu": w_branch_sgu[i], "w_out": w_out[i],
            "ffn_w_up": ffn_w_up[i], "ffn_conv_w": ffn_conv_w[i], "ffn_conv_b": ffn_conv_b[i],
            "ffn_w_down": ffn_w_down[i],
        }
        mod_ctx = jax.nn.silu(c_ctx)[None, :] @ w_mod[i] + b_mod[i]
        mod_lat = jax.nn.silu(c) @ w_mod[i] + b_mod[i]
        zero_state = jnp.zeros((xp.shape[0], SSD_HEADS, SSD_HEADDIM, SSD_STATE), jnp.float32)
        xp, hf, hb = _trunk_layer(xp, mod_ctx, zero_state, zero_state, False, lp)
        new_f.append(hf)
        new_b.append(hb)
        xs, _, _ = _trunk_layer(xs, mod_lat, state_ssd_fwd[:, i], state_ssd_bwd[:, i], True, lp)
    new_state_ssd_fwd = jnp.stack(new_f, axis=1).astype(x_prompt.dtype)
    new_state_ssd_bwd = jnp.stack(new_b, axis=1).astype(x_prompt.dtype)
    return (xp, xs, new_state_ssd_fwd, new_state_ssd_bwd)
```

```python
import types
import numpy as np
from contextlib import ExitStack
import concourse.bass as bass
import concourse.mybir as mybir
from concourse.bass_utils import run_bass_kernel_spmd

F32 = mybir.dt.float32
BF16 = mybir.dt.bfloat16
AF = mybir.ActivationFunctionType
ALU = mybir.AluOpType

ENGS = ("pe", "act", "dve", "pool", "sp")
D = 1024
NT = 2048
NCH = 16
EPS = 1e-6
DFF = 2816
NFB = 22
R_C, R_NMP, R_NFP, R_SH1, R_SC1, R_SH2, R_SC2, R_DTB, R_ALOG, R_SCONV, R_FCONV = 0, 8, 16, 24, 32, 40, 48, 56, 57, 58, 154
R_SSDN = 594
NROWS = 640
B_NMPOST, B_NFPOST, B_G1, B_G2, B_SGW, B_SGB, B_SSDN, B_SSDD, NRB = 0, 1024, 2048, 3072, 4096, 5120, 6144, 8192, 8224


def _freeze(fn):
    if fn is None or fn.__closure__ is None:
        return fn
    cells = []
    for c in fn.__closure__:
        try:
            cells.append(types.CellType(c.cell_contents))
        except ValueError:
            cells.append(c)
    return types.FunctionType(fn.__code__, fn.__globals__, fn.__name__, fn.__defaults__, tuple(cells))


class Sched:
    def __init__(self):
        self.q = {e: [] for e in ENGS}
        self.cnt = {e: 0 for e in ENGS}
        self.seen = {e: {} for e in ENGS}
        self.lastw = {}
        self.readers = {}
        self.dsems = set()

    def new_dma_sem(self, name):
        self.cnt[name] = 0
        self.dsems.add(name)
        return name

    def _need(self, eng, sem, val, waits):
        if val <= 0 or (eng == "pe" and sem == "pe"):
            return
        if sem in self.dsems:
            val = self.cnt[sem]
        if self.seen[eng].get(sem, 0) >= val:
            return
        self.seen[eng][sem] = val
        waits[sem] = max(waits.get(sem, 0), val)

    def _deps(self, eng, reads, writes):
        waits = {}
        for k in reads:
            w = self.lastw.get(k)
            if w is not None:
                self._need(eng, w[0], w[1], waits)
        for k in writes:
            w = self.lastw.get(k)
            if w is not None:
                self._need(eng, w[0], w[1], waits)
            for r in self.readers.get(k, ()):
                self._need(eng, r[0], r[1], waits)
        return waits

    def _record(self, ev, reads, writes):
        for k in reads:
            lst = self.readers.setdefault(k, [])
            lst.append(ev)
            if len(lst) > 48:
                d = {}
                for s, v in lst:
                    d[s] = max(d.get(s, 0), v)
                self.readers[k] = list(d.items())
        for k in writes:
            self.lastw[k] = ev
            self.readers[k] = []

    def op(self, eng, fn, reads=(), writes=(), inc=True):
        waits = self._deps(eng, reads, writes)
        if inc:
            self.cnt[eng] += 1
            ev = (eng, self.cnt[eng])
            self.q[eng].append((waits, _freeze(fn), eng, 1))
        else:
            ev = (eng, self.cnt[eng] + 1)
            self.q[eng].append((waits, _freeze(fn), eng, 0))
        self._record(ev, reads, writes)

    def dma(self, qeng, dsem, fn, reads=(), writes=()):
        waits = self._deps(qeng, reads, writes)
        self.cnt[dsem] += 16
        ev = (dsem, self.cnt[dsem])
        self.q[qeng].append((waits, _freeze(fn), dsem, 16))
        self._record(ev, reads, writes)

    def barrier(self, exclude=()):
        for e in ENGS:
            waits = {}
            for s, v in self.cnt.items():
                if s in exclude:
                    continue
                self._need(e, s, v, waits)
            if waits:
                self.q[e].append((waits, None, None, 0))

    def final_wait(self, eng="sp"):
        waits = {}
        for s, v in self.cnt.items():
            self._need(eng, s, v, waits)
        self.q[eng].append((waits, None, None, 0))

    def emit(self, nc, stack):
        sems = {s: stack.enter_context(nc.semaphore("s_" + s)) for s in self.cnt}
        block = stack.enter_context(nc.Block())
        hmap = {"pe": block.tensor, "act": block.scalar, "dve": block.vector,
                "pool": block.gpsimd, "sp": block.sync}
        for e in ENGS:
            items = self.q[e]

            def body(engh, items=items):
                for waits, fn, sem, inc in items:
                    for s, v in waits.items():
                        engh.wait_ge(sems[s], v)
                    if fn is not None:
                        ins = fn(engh)
                        if inc:
                            ins.then_inc(sems[sem], inc)
            hmap[e](body)


class Arena:
    def __init__(self, nc, stack, nbytes):
        self.t16 = stack.enter_context(nc.sbuf_tensor("arena", [128, nbytes // 2], BF16))
        self.t32 = self.t16.bitcast(F32)
        self.nbytes = nbytes
        self.off = 0

    def alloc(self, ncols, dtype, parts=128):
        sz = 2 if dtype == BF16 else 4
        nb = (ncols * sz + 63) // 64 * 64
        off = self.off
        if off + nb > getattr(self, "top", self.nbytes):
            raise RuntimeError(f"arena OOM: need {off + nb} > {getattr(self, 'top', self.nbytes)}")
        self.off += nb
        if dtype == BF16:
            return self.t16[0:parts, off // 2: off // 2 + ncols]
        return self.t32[0:parts, off // 4: off // 4 + ncols]

    def alloc_top(self, ncols, dtype, parts=128):
        sz = 2 if dtype == BF16 else 4
        nb = (ncols * sz + 63) // 64 * 64
        self.top = getattr(self, "top", self.nbytes) - nb
        off = self.top
        if dtype == BF16:
            return self.t16[0:parts, off // 2: off // 2 + ncols]
        return self.t32[0:parts, off // 4: off // 4 + ncols]


def build_program(stop_after=None):
    dbg = stop_after is not None
    nc = bass.Bass("TRN2", target_bir_lowering=False)

    def din(name, shape, dt=F32):
        return nc.dram_tensor(name, list(shape), dt, kind="ExternalInput").ap()

    xin = din("xin", [NT, D])
    rows_d = din("rows", [NROWS, 128])
    rowb_d = din("rowb", [1, NRB])
    pc_d = din("pc", [128, 66])
    h0_d = din("h0", [2, 2048, 128])
    wmod_d = din("wmod", [6, D, D])
    wA_d = din("wA", [D, 3072])
    wB_d = din("wB", [4, D, 1280])
    wdt_d = din("wdt", [D, 64])
    wga_d = din("wga", [D, D])
    wsT_d = din("wsT", [128, 8, 128])
    sgub_d = din("sgub", [1, 1024])
    wbg_d = din("wbg", [D, D])
    wbs_d = din("wbs", [2048, D])
    wout_d = din("wout", [D, D])
    wup_d = din("wup", [D, 2 * DFF])
    wdn_d = din("wdn", [DFF, D])
    yout = nc.dram_tensor("yout", [NT, D], F32, kind="ExternalOutput").ap()
    stf_d = nc.dram_tensor("stf", [8, 2048, 128], F32, kind="ExternalOutput").ap()
    stb_d = nc.dram_tensor("stb", [8, 2048, 128], F32, kind="ExternalOutput").ap()
    skind = dict(kind="ExternalOutput") if dbg else {}
    msgu_d = nc.dram_tensor("msgu_scr", [NT, D], BF16, **skind).ap()
    yg_d = nc.dram_tensor("yg_scr", [NT, 2048], BF16, **skind).ap()
    st_d = (stf_d, stb_d)

    S = Sched()
    op, dma = S.op, S.dma
    taps = {}

    def dtap(name, ap, key, dt):
        if not dbg:
            return
        shp = [ap.shape[0], int(np.prod(ap.shape[1:]))]
        dd = nc.dram_tensor("dbg_" + name, shp, dt, kind="ExternalOutput").ap()
        ds_ = S.new_dma_sem("dbg_" + name)
        dma("sp", ds_, lambda e: e.dma_start(out=dd, in_=ap), reads=[key] if isinstance(key, str) else key)

    class _Stop(Exception):
        pass

    def rowbc(off, n):
        return bass.AP(rowb_d.tensor, off, [[0, 128], [1, n]])

    with ExitStack() as st:
        A = Arena(nc, st, 200 * 1024)
        psh = [st.enter_context(nc.psum_tensor(f"ps{i}", [128, 512], F32)) for i in range(8)]
        ps = [t[:] for t in psh]
        psb = [t.bitcast(BF16)[:] for t in psh]
        PS = [f"ps{i}" for i in range(8)]
        nsem = [0]

        def dsem():
            nsem[0] += 1
            return S.new_dma_sem(f"d{nsem[0]}")

        hT = A.alloc(8 * NT, BF16)
        hTv = hT.rearrange("p (k t) -> p k t", k=8)
        featc = A.alloc(NROWS, F32)
        pcs = A.alloc(66, F32)
        id32 = A.alloc(128, F32)
        id16 = A.alloc(128, BF16)
        ones16 = A.alloc(128, BF16)
        modT = A.alloc(32, F32)
        g1 = A.alloc(8, F32)
        g2 = A.alloc(8, F32)
        gw1 = A.alloc(1024, F32)
        gw2 = A.alloc(1024, F32)
        ssy = A.alloc(64, F32)
        tiny = A.alloc(64, F32)
        mhalf = A.alloc(1, F32)
        P_MARK = A.off
        keepf, keepb, nfs, nff = pcs[:, 0:16], pcs[:, 16:32], pcs[:, 32:33], pcs[:, 33:66]

        def hk(c):
            return f"hT{c}"
        HT_ALL = [hk(c) for c in range(NCH)]

        dc = dsem()
        dma("sp", dc, lambda e: e.dma_start(out=pcs, in_=pc_d), writes=["pcs"])
        op("pool", lambda e: e.memset(id32, 0.0), writes=["id32"])
        op("pool", lambda e: e.affine_select(out=id32, in_=id32, pattern=[[-1, 128]], compare_op=ALU.not_equal,
                                             fill=1.0, base=0, channel_multiplier=1), reads=["id32"], writes=["id32"])
        op("pool", lambda e: e.tensor_copy(id16, id32), reads=["id32"], writes=["id16"])
        op("pool", lambda e: e.memset(ones16, 1.0), writes=["ones16"])
        op("pool", lambda e: e.memset(ssy, 0.0), writes=["ssy"])
        op("pool", lambda e: e.memset(mhalf, -0.5), writes=["mhalf"])
        rst = A.alloc(128, F32)
        drow = dsem()
        for b in range(NROWS // 128):
            dma("sp", drow, lambda e, b=b: e.dma_start(out=rst, in_=rows_d[b * 128:(b + 1) * 128, :]), writes=["rst"])
            op("pe", lambda e: e.transpose(ps[0][:, 0:128], rst, id32), reads=["rst", "id32"], writes=[PS[0]])
            op("dve", lambda e, b=b: e.tensor_copy(featc[:, b * 128:(b + 1) * 128], ps[0][:, 0:128]), reads=[PS[0]], writes=["featc"])
        scT = A.alloc(8, F32)
        scT16 = A.alloc(8, BF16)
        scB = A.alloc(8 * 128, BF16)
        op("act", lambda e: e.activation(out=scT, in_=featc[:, R_C:R_C + 8], func=AF.Silu), reads=["featc"], writes=["scT"])
        op("dve", lambda e: e.tensor_copy(scT16, scT), reads=["scT"], writes=["scT16"])
        for k in range(8):
            op("dve", lambda e, k=k: e.tensor_copy(scB[:, k * 128:(k + 1) * 128], scT[:, k:k + 1].broadcast_to([128, 128])),
               reads=["scT"], writes=["scB"])
        wmb = [A.alloc(8 * 1024, BF16) for _ in range(2)]
        gb_t = A.alloc(1024, F32)
        dwm = [dsem(), dsem()]
        SECS = ((0, 0), (1, 1), (2, "g1"), (3, 2), (4, 3), (5, "g2"))

        def mod_section(i):
            sec, kind = SECS[i]
            sw = i % 2
            wmv = wmb[sw].rearrange("p (k n) -> p k n", k=8)
            dma("pool", dwm[sw], lambda e: e.dma_start(out=wmv, in_=wmod_d[sec].rearrange("(k p) n -> p k n", p=128)), writes=[f"wm{sw}"])
            if isinstance(kind, int):
                for j in range(8):
                    for k in range(8):
                        op("pe", lambda e, j=j, k=k: e.matmul(ps[1][:, j:j + 1], lhsT=wmv[:, k, j * 128:(j + 1) * 128], rhs=scT16[:, k:k + 1],
                                                              start=(k == 0), stop=(k == 7)), reads=[f"wm{sw}", "scT16"], writes=[PS[1]], inc=(k == 7))
                rb = (R_SH1, R_SC1, R_SH2, R_SC2)[kind]
                op("dve", lambda e: e.tensor_tensor(out=modT[:, kind * 8:kind * 8 + 8], in0=ps[1][:, 0:8], in1=featc[:, rb:rb + 8], op=ALU.add),
                   reads=[PS[1], "featc"], writes=[f"modT{kind}"])
            else:
                boff, npo, gw = (B_G1, B_NMPOST, gw1) if kind == "g1" else (B_G2, B_NFPOST, gw2)
                for nb in range(2):
                    for k in range(8):
                        op("pe", lambda e, nb=nb, k=k: e.matmul(ps[2 + nb], lhsT=scB[:, k * 128:(k + 1) * 128], rhs=wmv[:, k, nb * 512:(nb + 1) * 512],
                                                                start=(k == 0), stop=(k == 7)), reads=[f"wm{sw}", "scB"], writes=[PS[2 + nb]], inc=(k == 7))
                dma("sp", dc, lambda e: e.dma_start(out=gb_t, in_=rowbc(boff, 1024)), writes=["gb_t"])
                dma("sp", dc, lambda e: e.dma_start(out=gw, in_=rowbc(npo, 1024)), writes=[kind])
                for nb in range(2):
                    sl = slice(nb * 512, (nb + 1) * 512)
                    op("dve", lambda e, nb=nb, sl=sl: e.tensor_tensor(out=gb_t[:, sl], in0=ps[2 + nb], in1=gb_t[:, sl], op=ALU.add),
                       reads=[PS[2 + nb], "gb_t"], writes=["gb_t"])
                op("dve", lambda e: e.tensor_tensor(out=gw, in0=gw, in1=gb_t, op=ALU.mult), reads=["gb_t", kind], writes=[kind])

        def make_g(gg, rb, so, kinds):
            op("dve", lambda e: e.tensor_scalar(out=gg, in0=modT[:, so:so + 8], scalar1=1.0, scalar2=None, op0=ALU.add),
               reads=[f"modT{kinds}"], writes=[f"g12_{so}"])
            op("dve", lambda e: e.tensor_tensor(out=gg, in0=gg, in1=featc[:, rb:rb + 8], op=ALU.mult),
               reads=[f"g12_{so}", "featc"], writes=[f"g12_{so}"])

        rmode = ["act"]

        def rstd_from_ss(ss_ap, n, out_ap, key):
            if rmode[0] == "pool":
                op("pool", lambda e: e.tensor_scalar(out=out_ap, in0=ss_ap, scalar1=1.0 / n, scalar2=EPS, op0=ALU.mult, op1=ALU.add),
                   reads=[key], writes=[key])
                op("pool", lambda e: e.tensor_tensor(out=out_ap, in0=out_ap, in1=mhalf[:, 0:1].broadcast_to(list(out_ap.shape)), op=ALU.pow),
                   reads=[key, "mhalf"], writes=[key])
            else:
                op("dve", lambda e: e.tensor_scalar(out=out_ap, in0=ss_ap, scalar1=1.0 / n, scalar2=EPS, op0=ALU.mult, op1=ALU.add),
                   reads=[key], writes=[key])
                op("act", lambda e: e.activation(out=out_ap, in_=out_ap, func=AF.Sqrt), reads=[key], writes=[key])
                op("dve", lambda e: e.reciprocal(out_ap, out_ap), reads=[key], writes=[key])

        mod_section(0)
        mod_section(1)
        make_g(g1, R_NMP, 8, 1)
        xb = [A.alloc(1024, F32) for _ in range(3)]
        xs = [A.alloc(1024, F32) for _ in range(2)]
        tn0 = A.alloc(NCH, F32)
        dx = [dsem(), dsem(), dsem()]

        def p0_P(c):
            s3, s = c % 3, c % 2
            tk = f"tn0_{c}"
            sscol = tn0[:, c:c + 1]
            dma("sp", dx[s3], lambda e: e.dma_start(out=xb[s3], in_=xin[c * 128:(c + 1) * 128, :]), writes=[f"xb{s3}"])
            op("act", lambda e: e.activation(out=xs[s], in_=xb[s3], func=AF.Square, accum_out=sscol), reads=[f"xb{s3}"], writes=[f"xs{s}", tk])
            rstd_from_ss(sscol, 1024, sscol, tk)
            op("dve", lambda e: e.tensor_scalar(out=xs[s], in0=xb[s3], scalar1=sscol, scalar2=None, op0=ALU.mult), reads=[f"xb{s3}", tk], writes=[f"xs{s}"])

        def p0_Q(c):
            s = c % 2
            pA, pB = 4 + 2 * s, 5 + 2 * s
            for k in range(8):
                pp = pA if k < 4 else pB
                op("pe", lambda e, k=k, pp=pp: e.transpose(ps[pp][:, (k % 4) * 128:(k % 4 + 1) * 128], xs[s][:, k * 128:(k + 1) * 128], id32),
                   reads=[f"xs{s}", "id32"], writes=[PS[pp]])
            for k in range(8):
                pp = pA if k < 4 else pB
                op("act", lambda e, k=k, pp=pp: e.activation(out=hTv[:, k, c * 128:(c + 1) * 128], in_=ps[pp][:, (k % 4) * 128:(k % 4 + 1) * 128],
                                                             func=AF.Identity, scale=g1[:, k:k + 1], bias=modT[:, k:k + 1]),
                   reads=[PS[pp], "g12_8", "modT0"], writes=[hk(c)])

        p0_P(0)
        nsec = 2
        for c in range(NCH):
            if c + 1 < NCH:
                p0_P(c + 1)
            p0_Q(c)
            if c % 4 == 1 and nsec < 6:
                mod_section(nsec)
                nsec += 1
        while nsec < 6:
            mod_section(nsec)
            nsec += 1
        make_g(g2, R_NFP, 24, 3)

        wA = A.alloc_top(8 * 3072, BF16)
        wAv = wA.rearrange("p (k n) -> p k n", k=8)
        wbg = A.alloc_top(8 * 1024, BF16)
        wbgv = wbg.rearrange("p (k n) -> p k n", k=8)
        wsT = A.alloc_top(8 * 128, BF16)
        wsTv = wsT.rearrange("p (g i) -> p g i", g=8)
        sgub = A.alloc_top(1024, BF16, parts=1)
        sgw = A.alloc_top(1024, F32)
        sgb = A.alloc_top(1024, F32)
        assert A.top >= A.off, (A.top, A.off)
        dwa, dwa2, dwa3 = dsem(), dsem(), dsem()
        dma("pool", dwa, lambda e: e.dma_start(out=wAv[:, :, 1024:2048], in_=wA_d[:, 1024:2048].rearrange("(k p) n -> p k n", p=128)), writes=["wAv"])
        dma("pool", dwa2, lambda e: e.dma_start(out=wAv[:, :, 0:1024], in_=wA_d[:, 0:1024].rearrange("(k p) n -> p k n", p=128)), writes=["wAu"])
        dma("pool", dwa2, lambda e: e.dma_start(out=wAv[:, :, 2048:3072], in_=wA_d[:, 2048:3072].rearrange("(k p) n -> p k n", p=128)), writes=["wAg"])
        dma("pool", dwa3, lambda e: e.dma_start(out=wbgv, in_=wbg_d.rearrange("(k p) n -> p k n", p=128)), writes=["wbg"])
        dma("pool", dwa3, lambda e: e.dma_start(out=wsTv, in_=wsT_d), writes=["wsT"])
        dma("pool", dwa3, lambda e: e.dma_start(out=sgub, in_=sgub_d), writes=["sgub"])
        dma("sp", dwa3, lambda e: e.dma_start(out=sgw, in_=rowbc(B_SGW, 1024)), writes=["sgw"])
        dma("sp", dwa3, lambda e: e.dma_start(out=sgb, in_=rowbc(B_SGB, 1024)), writes=["sgb"])

        if stop_after == "0":
            S.barrier()
            dtap("hT", hT, HT_ALL, BF16)
            dtap("modT", modT, "featc", F32)
            dtap("gw1", gw1, "g1", F32)
            S.final_wait("sp")
            S.emit(nc, st)
            return nc
        S.barrier(exclude=(dwa, dwa2, dwa3))
        A.off = P_MARK
        vh = [A.alloc(1024, F32) for _ in range(2)]
        vn = [A.alloc(1024, BF16) for _ in range(2)]
        uT = [A.alloc(1024, BF16) for _ in range(2)]
        ysT = [A.alloc(1024, BF16) for _ in range(2)]
        sgg = [A.alloc(1024, F32) for _ in range(2)]
        msg = [A.alloc(1024, BF16) for _ in range(2)]
        bstA = A.alloc(NCH * 12, F32)
        mvA = A.alloc(NCH * 2, F32)
        dms = [dsem(), dsem()]

        def a_X(c):
            s = c % 2
            ct = slice(c * 128, (c + 1) * 128)
            bst = bstA[:, c * 12:(c + 1) * 12]
            mv = mvA[:, c * 2:(c + 1) * 2]
            mk = f"mv{c}"
            for nb in range(2):
                for k in range(8):
                    op("pe", lambda e, nb=nb, k=k: e.matmul(ps[nb], lhsT=hTv[:, k, ct], rhs=wAv[:, k, 1024 + nb * 512:1024 + (nb + 1) * 512],
                                                            start=(k == 0), stop=(k == 7)), reads=[hk(c), "wAv"], writes=[PS[nb]], inc=(k == 7))
            for nb in range(2):
                op("dve", lambda e, nb=nb: e.bn_stats(bst[:, nb * 6:(nb + 1) * 6], ps[nb]), reads=[PS[nb]], writes=[mk])
            op("dve", lambda e: e.bn_aggr(mv, bst), reads=[mk], writes=[mk])
            op("dve", lambda e: e.tensor_scalar(out=mv[:, 1:2], in0=mv[:, 1:2], scalar1=EPS, scalar2=None, op0=ALU.add), reads=[mk], writes=[mk])
            op("act", lambda e: e.activation(out=mv[:, 1:2], in_=mv[:, 1:2], func=AF.Sqrt), reads=[mk], writes=[mk])
            op("dve", lambda e: e.reciprocal(mv[:, 1:2], mv[:, 1:2]), reads=[mk], writes=[mk])
            for nb in range(2):
                op("dve", lambda e, nb=nb: e.tensor_scalar(out=vh[s][:, nb * 512:(nb + 1) * 512], in0=ps[nb], scalar1=mv[:, 0:1], scalar2=mv[:, 1:2],
                                                            op0=ALU.subtract, op1=ALU.mult), reads=[PS[nb], mk], writes=[f"vh{s}"])
            for blk in range(8):
                o = ps[4 + blk // 4][:, (blk % 4) * 128:(blk % 4 + 1) * 128]
                for k in range(8):
                    op("pe", lambda e, blk=blk, k=k, o=o: e.matmul(o, lhsT=wAv[:, k, blk * 128:(blk + 1) * 128], rhs=hTv[:, k, ct],
                                                                  start=(k == 0), stop=(k == 7)), reads=[hk(c), "wAu"], writes=[PS[4 + blk // 4]], inc=(k == 7))
            for hb in range(2):
                op("act", lambda e, hb=hb: e.copy(uT[s][:, hb * 512:(hb + 1) * 512], ps[4 + hb]), reads=[PS[4 + hb]], writes=[f"uT{s}"])
            for nb in range(2):
                for k in range(8):
                    op("pe", lambda e, nb=nb, k=k: e.matmul(ps[6 + nb], lhsT=hTv[:, k, ct], rhs=wAv[:, k, 2048 + nb * 512:2048 + (nb + 1) * 512],
                                                            start=(k == 0), stop=(k == 7)), reads=[hk(c), "wAg"], writes=[PS[6 + nb]], inc=(k == 7))
                op("act", lambda e, nb=nb: e.activation(out=sgg[s][:, nb * 512:(nb + 1) * 512], in_=ps[6 + nb], func=AF.Sigmoid),
                   reads=[PS[6 + nb]], writes=[f"sgg{s}"])
            op("dve", lambda e: e.tensor_tensor(out=vh[s], in0=vh[s], in1=sgw, op=ALU.mult), reads=[f"vh{s}", "sgw"], writes=[f"vh{s}"])
            op("dve", lambda e: e.tensor_tensor(out=vn[s], in0=vh[s], in1=sgb, op=ALU.add), reads=[f"vh{s}", "sgb"], writes=[f"vn{s}"])

        def a_Y(c):
            s = c % 2
            ysTv = ysT[s].rearrange("p (g i) -> p g i", g=8)
            for g in range(8):
                o = ps[2 + g // 4][:, (g % 4) * 128:(g % 4 + 1) * 128]
                op("pe", lambda e, g=g, o=o: e.matmul(o, lhsT=vn[s][:, g * 128:(g + 1) * 128], rhs=wsTv[:, g, :], start=True, stop=False),
                   reads=[f"vn{s}", "wsT"], writes=[PS[2 + g // 4]], inc=False)
                op("pe", lambda e, g=g, o=o: e.matmul(o, lhsT=ones16[0:1, :], rhs=sgub[0:1, g * 128:(g + 1) * 128], start=False, stop=True),
                   reads=["ones16", "sgub"], writes=[PS[2 + g // 4]], inc=True)
            for hb in range(2):
                op("dve", lambda e, hb=hb: e.tensor_tensor(out=ysT[s][:, hb * 512:(hb + 1) * 512], in0=ps[2 + hb], in1=uT[s][:, hb * 512:(hb + 1) * 512], op=ALU.mult),
                   reads=[PS[2 + hb], f"uT{s}"], writes=[f"ysT{s}"])
            for nb in range(2):
                for g in range(8):
                    op("pe", lambda e, nb=nb, g=g: e.matmul(ps[2 + nb], lhsT=ysTv[:, g, :], rhs=wbgv[:, g, nb * 512:(nb + 1) * 512],
                                                            start=(g == 0), stop=(g == 7)), reads=[f"ysT{s}", "wbg"], writes=[PS[2 + nb]], inc=(g == 7))
                op("dve", lambda e, nb=nb: e.tensor_tensor(out=msg[s][:, nb * 512:(nb + 1) * 512], in0=ps[2 + nb], in1=sgg[s][:, nb * 512:(nb + 1) * 512], op=ALU.mult),
                   reads=[PS[2 + nb], f"sgg{s}"], writes=[f"msg{s}"])
            dma("sp", dms[s], lambda e: e.dma_start(out=msgu_d[c * 128:(c + 1) * 128, :], in_=msg[s]), reads=[f"msg{s}"], writes=[f"msgu_d{c}"])

        a_X(0)
        for c in range(NCH):
            if c + 1 < NCH:
                a_X(c + 1)
            a_Y(c)

        if stop_after == "A":
            S.final_wait("sp")
            S.emit(nc, st)
            return nc
        S.barrier()
        A.off = P_MARK
        A.top = A.nbytes
        hi16 = A.alloc(NT, BF16)
        lo16 = A.alloc(NT, BF16)
        btm = A.alloc(NCH * 64, F32)
        btmv = btm.rearrange("p (c r) -> p c r", c=NCH)
        e64 = A.alloc(64, BF16)
        Acol = A.alloc(1, F32)
        Db = A.alloc(32, F32)
        trif = A.alloc(128, F32)
        trib = A.alloc(128, F32)
        kbias = A.alloc(32, F32)
        B0_MARK = A.off
        wdt = A.alloc(8 * 64, BF16)
        wdtv = wdt.rearrange("p (k n) -> p k n", k=8)
        dtT = A.alloc(NT, F32)
        aT = A.alloc(NT, F32)
        acs = A.alloc(NT, F32)
        t2 = A.alloc(NT, F32)
        nstart = A.alloc(NT, F32)
        dwd = dsem()
        dma("pool", dwd, lambda e: e.dma_start(out=wdtv, in_=wdt_d.rearrange("(k p) n -> p k n", p=128)), writes=["wdt"])
        dma("sp", dc, lambda e: e.dma_start(out=Db, in_=rowbc(B_SSDD, 32)), writes=["Db"])
        op("dve", lambda e: e.tensor_scalar(out=kbias, in0=pcs[:, 0:32], scalar1=-1.0, scalar2=1.0e4, op0=ALU.add, op1=ALU.mult), reads=["pcs"], writes=["kbias"])
        op("pool", lambda e: e.memset(e64, 0.0), writes=["e64"])
        op("pool", lambda e: e.affine_select(out=e64[0:64, :], in_=e64[0:64, :], pattern=[[-1, 64]], compare_op=ALU.not_equal, fill=1.0,
                                             base=0, channel_multiplier=1), reads=["e64"], writes=["e64"])
        op("pool", lambda e: e.memset(trif, 1.0), writes=["trif"])
        op("pool", lambda e: e.affine_select(out=trif, in_=trif, pattern=[[1, 128]], compare_op=ALU.is_ge, fill=0.0, base=0, channel_multiplier=-1),
           reads=["trif"], writes=["trif"])
        op("pool", lambda e: e.memset(trib, 1.0), writes=["trib"])
        op("pool", lambda e: e.affine_select(out=trib, in_=trib, pattern=[[-1, 128]], compare_op=ALU.is_ge, fill=0.0, base=0, channel_multiplier=1),
           reads=["trib"], writes=["trib"])
        op("pool", lambda e: e.memset(nstart, 1.0), writes=["nstart"])
        op("pool", lambda e: e.memset(nstart[:, 0:NT:128], 0.0), reads=["nstart"], writes=["nstart"])
        op("act", lambda e: e.activation(out=Acol[0:64, :], in_=featc[0:64, R_ALOG:R_ALOG + 1], func=AF.Exp), reads=["featc"], writes=["Acol"])
        op("dve", lambda e: e.tensor_scalar(out=Acol[0:64, :], in0=Acol[0:64, :], scalar1=-1.0, scalar2=None, op0=ALU.mult), reads=["Acol"], writes=["Acol"])
        for t in range(4):
            ts = slice(t * 512, (t + 1) * 512)
            for k in range(8):
                op("pe", lambda e, t=t, k=k, ts=ts: e.matmul(ps[t][0:64, :], lhsT=wdtv[:, k, :], rhs=hTv[:, k, ts], start=(k == 0), stop=(k == 7)),
                   reads=HT_ALL[4 * t:4 * t + 4] + ["wdt"], writes=[PS[t]], inc=(k == 7))
            op("act", lambda e, t=t, ts=ts: e.activation(out=dtT[0:64, ts], in_=ps[t][0:64, :], func=AF.Exp, bias=featc[0:64, R_DTB:R_DTB + 1]),
               reads=[PS[t], "featc"], writes=["dtT"])
        op("act", lambda e: e.activation(out=dtT[0:64, :], in_=dtT[0:64, :], func=AF.Ln, bias=1.0), reads=["dtT"], writes=["dtT"])
        op("dve", lambda e: e.tensor_scalar(out=aT[0:64, :], in0=dtT[0:64, :], scalar1=Acol[0:64, 0:1], scalar2=None, op0=ALU.mult),
           reads=["dtT", "Acol"], writes=["aT"])
        op("dve", lambda e: e.tensor_tensor_scan(out=acs[0:64, :], data0=nstart[0:64, :], data1=aT[0:64, :], initial=0.0, op0=ALU.mult, op1=ALU.add),
           reads=["nstart", "aT"], writes=["acs"])
        acs3 = acs.rearrange("p (c i) -> p c i", c=NCH)
        t23 = t2.rearrange("p (c i) -> p c i", c=NCH)
        op("dve", lambda e: e.tensor_tensor(out=t2[32:64, :], in0=aT[32:64, :], in1=acs[32:64, :], op=ALU.subtract), reads=["aT", "acs"], writes=["t2"])
        op("dve", lambda e: e.tensor_tensor(out=t23[32:64], in0=t23[32:64], in1=acs3[32:64, :, 127:128].broadcast_to([32, NCH, 128]), op=ALU.add),
           reads=["t2", "acs"], writes=["t2"])
        op("dve", lambda e: e.tensor_copy(acs[32:64, :], t2[32:64, :]), reads=["t2"], writes=["acs"])
        op("dve", lambda e: e.tensor_copy(hi16[0:64, :], acs[0:64, :]), reads=["acs"], writes=["hi16"])
        op("dve", lambda e: e.tensor_tensor(out=lo16[0:64, :], in0=acs[0:64, :], in1=hi16[0:64, :], op=ALU.subtract), reads=["acs", "hi16"], writes=["lo16"])
        op("act", lambda e: e.activation(out=t2[0:64, :], in_=dtT[0:64, :], func=AF.Ln), reads=["dtT", "t2"], writes=["t2"])
        op("dve", lambda e: e.tensor_tensor(out=t2[0:64, :], in0=t2[0:64, :], in1=acs[0:64, :], op=ALU.subtract), reads=["t2", "acs"], writes=["t2"])
        for c in range(NCH):
            pp = 4 + c % 2
            op("pe", lambda e, c=c, pp=pp: e.transpose(ps[pp][:, 0:64], t2[0:64, c * 128:(c + 1) * 128], id32[0:64, 0:64]),
               reads=["t2", "id32"], writes=[PS[pp]])
            op("dve", lambda e, c=c, pp=pp: e.tensor_copy(btmv[:, c, :], ps[pp][:, 0:64]), reads=[PS[pp]], writes=["btm"])

        if stop_after == "B0":
            dtap("acs", acs, "acs", F32)
            dtap("dtT", dtT, "dtT", F32)
            dtap("btm", btm, "btm", F32)
            dtap("hi16", hi16, "hi16", BF16)
            S.final_wait("sp")
            S.emit(nc, st)
            return nc
        for g in range(4):
            S.barrier()
            A.off = B0_MARK
            wBz = A.alloc(8 * 512, BF16)
            wBzv = wBz.rearrange("p (k n) -> p k n", k=8)
            BT = A.alloc(NT, BF16)
            CT = A.alloc(NT, BF16)
            xtm = A.alloc(NCH * 512, BF16)
            xtmv = xtm.rearrange("p (c n) -> p c n", c=NCH)
            Btm = A.alloc(NCH * 128, BF16)
            Btmv = Btm.rearrange("p (c n) -> p c n", c=NCH)
            yacc = A.alloc(NCH * 512, BF16)
            yaccv = yacc.rearrange("p (c n) -> p c n", c=NCH)
            dD = A.alloc(8 * 128, BF16)
            dDv = dD.rearrange("p (h i) -> p h i", h=8)
            CBmA = A.alloc(NCH * 2 * 128, BF16)
            CBmAv = CBmA.rearrange("p (c d i) -> p c d i", c=NCH, d=2)
            szA = A.alloc(NCH * 512, BF16)
            szAv = szA.rearrange("p (c n) -> p c n", c=NCH)
            H = [A.alloc(512, F32), A.alloc(512, F32)]
            G_MARK = A.off
            wBx = A.alloc(8 * 768, BF16)
            wBxv = wBx.rearrange("p (k n) -> p k n", k=8)
            stg = A.alloc(6 * 2050, BF16)
            stgv = stg.rearrange("p (b t) -> p b t", b=6)
            dg = A.alloc(6 * 3 * 128, BF16)
            dgv = dg.rearrange("p (b t i) -> p b t i", b=6, t=3)
            xfL = A.alloc(48, BF16)
            xfR = A.alloc(48, BF16)
            xfLv = xfL.rearrange("p (b m) -> p b m", b=6)
            xfRv = xfR.rearrange("p (b m) -> p b m", b=6)
            xc = [A.alloc(4 * 512, BF16) for _ in range(2)]
            hst = A.alloc(512, F32)
            fb = R_SCONV + g * 24
            dwb = dsem()
            dma("pool", dwb, lambda e, g=g: e.dma_start(out=wBxv, in_=wB_d[g, :, 512:1280].rearrange("(k p) n -> p k n", p=128)), writes=["wBx"])
            dma("pool", dwb, lambda e, g=g: e.dma_start(out=wBzv, in_=wB_d[g, :, 0:512].rearrange("(k p) n -> p k n", p=128)), writes=["wBz"])
            for blk in range(6):
                for tap in range(3):
                    col = fb + blk * 4 + tap
                    op("dve", lambda e, blk=blk, tap=tap, col=col: e.tensor_scalar(out=dgv[:, blk, tap, :], in0=id32, scalar1=featc[:, col:col + 1],
                                                                                   scalar2=None, op0=ALU.mult), reads=["id32", "featc"], writes=["dg"])
            for h in range(8):
                op("dve", lambda e, h=h: e.tensor_scalar(out=dDv[:, h, :], in0=id32, scalar1=Db[:, g * 8 + h:g * 8 + h + 1], scalar2=None, op0=ALU.mult),
                   reads=["id32", "Db"], writes=["dD"])
            op("pool", lambda e: e.memset(stgv[:, :, 0:1], 0.0), writes=["stg"])
            op("pool", lambda e: e.memset(stgv[:, :, 2049:2050], 0.0), writes=["stg"])
            dh = dsem()
            for d in range(2):
                dma("sp", dh, lambda e, d=d, g=g: e.dma_start(out=hst.rearrange("p (b n) -> p b n", b=4),
                                                             in_=h0_d[d, g * 512:(g + 1) * 512, :].rearrange("(b p) n -> p b n", p=128)), writes=["hst"])
                for b in range(4):
                    op("pe", lambda e, b=b: e.transpose(ps[7][:, b * 128:(b + 1) * 128], hst[:, b * 128:(b + 1) * 128], id32), reads=["hst", "id32"], writes=[PS[7]])
                op("dve", lambda e, d=d: e.tensor_copy(H[d], ps[7]), reads=[PS[7]], writes=[f"H{d}"])
            for t in range(4):
                ts = slice(t * 512, (t + 1) * 512)
                for blk in range(6):
                    pp = (t * 6 + blk) % 4
                    for k in range(8):
                        op("pe", lambda e, blk=blk, k=k, pp=pp, ts=ts: e.matmul(ps[pp], lhsT=wBxv[:, k, blk * 128:(blk + 1) * 128], rhs=hTv[:, k, ts],
                                                                              start=(k == 0), stop=(k == 7)), reads=HT_ALL[4 * t:4 * t + 4] + ["wBx"], writes=[PS[pp]], inc=(k == 7))
                    if blk % 2 == 0:
                        op("act", lambda e, blk=blk, pp=pp, t=t: e.copy(stgv[:, blk, 1 + t * 512:1 + (t + 1) * 512], ps[pp]), reads=[PS[pp]], writes=["stg"])
                    else:
                        op("dve", lambda e, blk=blk, pp=pp, t=t: e.tensor_copy(stgv[:, blk, 1 + t * 512:1 + (t + 1) * 512], ps[pp]), reads=[PS[pp]], writes=["stg"])
            op("dve", lambda e: e.tensor_scalar(out=xfLv, in0=stgv[:, :, 0:2048:256], scalar1=nfs, scalar2=None, op0=ALU.mult), reads=["stg", "pcs"], writes=["xfL"])
            op("dve", lambda e: e.tensor_scalar(out=xfRv, in0=stgv[:, :, 257:2050:256], scalar1=nfs, scalar2=None, op0=ALU.mult), reads=["stg", "pcs"], writes=["xfR"])
            for t in range(4):
                xs_ = t % 2
                xcv = xc[xs_].rearrange("p (b t) -> p b t", b=4)
                for blk in range(6):
                    pp = 4 + (t * 6 + blk) % 2
                    for tap in range(3):
                        op("pe", lambda e, blk=blk, tap=tap, pp=pp, t=t: e.matmul(ps[pp], lhsT=dgv[:, blk, tap, :], rhs=stgv[:, blk, t * 512 + tap:t * 512 + tap + 512],
                                                                                  start=(tap == 0), stop=False), reads=["dg", "stg"], writes=[PS[pp]], inc=False)
                    op("pe", lambda e, blk=blk, pp=pp, t=t: e.matmul(ps[pp][:, 0:512:256], lhsT=dgv[:, blk, 0, :], rhs=xfLv[:, blk, 2 * t:2 * t + 2], start=False, stop=False),
                       reads=["dg", "xfL"], writes=[PS[pp]], inc=False)
                    op("pe", lambda e, blk=blk, pp=pp, t=t: e.matmul(ps[pp][:, 255:512:256], lhsT=dgv[:, blk, 2, :], rhs=xfRv[:, blk, 2 * t:2 * t + 2], start=False, stop=True),
                       reads=["dg", "xfR"], writes=[PS[pp]], inc=True)
                    bcol = fb + blk * 4 + 3
                    if blk < 4:
                        dst, dk = xcv[:, blk, :], f"xc{xs_}"
                    elif blk == 4:
                        dst, dk = BT[:, t * 512:(t + 1) * 512], f"BT{t}"
                    else:
                        dst, dk = CT[:, t * 512:(t + 1) * 512], f"CT{t}"
                    op("act", lambda e, pp=pp, dst=dst, bcol=bcol: e.activation(out=dst, in_=ps[pp], func=AF.Silu, bias=featc[:, bcol:bcol + 1]),
                       reads=[PS[pp], "featc"], writes=[dk])
                for cc in range(4):
                    c = t * 4 + cc
                    ct = slice(c * 128, (c + 1) * 128)
                    for b in range(4):
                        op("pe", lambda e, b=b, cc=cc, xcv=xcv: e.transpose(psb[6][:, b * 128:(b + 1) * 128], xcv[:, b, cc * 128:(cc + 1) * 128], id16),
                           reads=[f"xc{xs_}", "id16"], writes=[PS[6]])
                    op("dve", lambda e, c=c: e.tensor_copy(xtmv[:, c, :], psb[6][:, 0:512]), reads=[PS[6]], writes=[f"xtm{c}"])
                    op("pe", lambda e, ct=ct: e.transpose(psb[7][:, 0:128], BT[:, ct], id16), reads=[f"BT{t}", "id16"], writes=[PS[7]])
                    op("act", lambda e, c=c: e.copy(Btmv[:, c, :], psb[7][:, 0:128]), reads=[PS[7]], writes=[f"Btm{c}"])
                    op("pe", lambda e, ct=ct: e.matmul(ps[7][:, 128:256], lhsT=BT[:, ct], rhs=CT[:, ct], start=True, stop=True), reads=[f"BT{t}", f"CT{t}"], writes=[PS[7]], inc=True)
                    op("dve", lambda e, c=c: e.tensor_tensor(out=CBmAv[:, c, 0, :], in0=ps[7][:, 128:256], in1=trif, op=ALU.mult), reads=[PS[7], "trif"], writes=[f"CBm{c}"])
                    op("dve", lambda e, c=c: e.tensor_tensor(out=CBmAv[:, c, 1, :], in0=ps[7][:, 128:256], in1=trib, op=ALU.mult), reads=[PS[7], "trib"], writes=[f"CBm{c}"])
                    pz = c % 4
                    for k in range(8):
                        op("pe", lambda e, k=k, ct=ct, pz=pz: e.matmul(ps[pz], lhsT=hTv[:, k, ct], rhs=wBzv[:, k, :], start=(k == 0), stop=(k == 7)),
                           reads=[hk(c), "wBz"], writes=[PS[pz]], inc=(k == 7))
                    op("act", lambda e, c=c, pz=pz: e.activation(out=szAv[:, c, :], in_=ps[pz], func=AF.Silu), reads=[PS[pz]], writes=[f"sz{c}"])

            S.barrier()
            A.off = G_MARK
            NS = 3
            sets = []
            for si in range(NS):
                d_ = {}
                for nm in ("Et", "MT", "Eo", "CE"):
                    d_[nm] = A.alloc(1024, BF16)
                    d_[nm + "v"] = d_[nm].rearrange("p (h i) -> p h i", h=8)
                d_["xw"] = A.alloc(512, BF16)
                d_["tD"] = A.alloc(512, F32)
                d_["cd"] = A.alloc(8, F32)
                d_["Hin"] = A.alloc(512, BF16)
                sets.append(d_)
            so = A.alloc(512, F32)
            ytb = [A.alloc(512, F32) for _ in range(2)]
            ygo = [A.alloc(512, BF16) for _ in range(2)]
            sqs = A.alloc(512, F32)
            dso = dsem()
            dyg = [dsem(), dsem()]
            units = []
            for k in range(NCH):
                units += [(k, 0), (NCH - 1 - k, 1)]
            visited = set()
            nfin = [0]
            nst = [0]
            pend = []
            firstv = {}
            Hs = [A.alloc(512, F32) for _ in range(2)]

            def frontPE(ui):
                c, d = units[ui]
                pa = ui % 2
                ct = slice(c * 128, (c + 1) * 128)
                r0 = d * 32 + g * 8
                pA = (2 * pa, 2 * pa + 1)
                for h in range(8):
                    bank = pA[h // 4]
                    o = ps[bank][:, (h % 4) * 128:(h % 4 + 1) * 128]
                    sel = e64[0:64, r0 + h:r0 + h + 1].broadcast_to([64, 128])
                    op("pe", lambda e, o=o, sel=sel: e.matmul(o, lhsT=sel, rhs=hi16[0:64, ct], start=True, stop=False), reads=["e64", "hi16"], writes=[PS[bank]], inc=False)
                    op("pe", lambda e, o=o, sel=sel: e.matmul(o, lhsT=sel, rhs=lo16[0:64, ct], start=False, stop=True), reads=["e64", "lo16"], writes=[PS[bank]], inc=True)

            def front(ui):
                c, d = units[ui]
                si, pa = ui % NS, ui % 2
                B = sets[si]
                ct = slice(c * 128, (c + 1) * 128)
                r0 = d * 32 + g * 8
                lastc = 127 if d == 0 else 0
                keep = (keepf if d == 0 else keepb)[:, c:c + 1]
                pA = (2 * pa, 2 * pa + 1)
                for hb in range(2):
                    bank = pA[hb]
                    op("act", lambda e, hb=hb, bank=bank: e.activation(out=B["Eo"][:, hb * 512:(hb + 1) * 512], in_=ps[bank], func=AF.Exp), reads=[PS[bank]], writes=[f"Eo{si}"])
                    op("act", lambda e, hb=hb, bank=bank: e.activation(out=B["cd"][:, hb * 4:(hb + 1) * 4], in_=ps[bank][:, lastc:512:128], func=AF.Exp,
                                                                       bias=kbias[:, d * 16 + c:d * 16 + c + 1]),
                       reads=[PS[bank], "kbias"], writes=[f"cd{si}"])
                op("dve", lambda e: e.tensor_tensor(out=B["tD"].rearrange("p (h i) -> p h i", h=4), in0=ps[pA[1]].rearrange("p (h i) -> p h i", h=4),
                                                    in1=btmv[:, c, r0 + 4:r0 + 8].unsqueeze(2).broadcast_to([128, 4, 128]), op=ALU.add),
                   reads=[PS[pA[1]], "btm"], writes=[f"tD{si}"])
                for h in range(4):
                    bank = pA[0]
                    o = ps[bank][:, h * 128:(h + 1) * 128]
                    op("act", lambda e, h=h, o=o: e.activation(out=B["Etv"][:, h, :], in_=o, func=AF.Exp, bias=btmv[:, c, r0 + h:r0 + h + 1]),
                       reads=[PS[bank], "btm"], writes=[f"Et{si}"])
                op("act", lambda e: e.activation(out=B["Et"][:, 512:1024], in_=B["tD"], func=AF.Exp), reads=[f"tD{si}"], writes=[f"Et{si}"])
                op("dve", lambda e: e.tensor_tensor(out=B["CEv"], in0=B["Eov"], in1=CT[:, ct].unsqueeze(1).broadcast_to([128, 8, 128]), op=ALU.mult),
                   reads=[f"Eo{si}", f"CT{c // 4}"], writes=[f"CE{si}"])
                op("dve", lambda e: e.tensor_scalar(out=B["Et"], in0=B["Et"], scalar1=1e30, scalar2=None, op0=ALU.min), reads=[f"Et{si}"], writes=[f"Et{si}"])
                op("dve", lambda e: e.tensor_tensor(out=B["MTv"], in0=B["Etv"], in1=CBmAv[:, c, d, :].unsqueeze(1).broadcast_to([128, 8, 128]), op=ALU.mult),
                   reads=[f"Et{si}", f"CBm{c}"], writes=[f"MT{si}"])
                op("dve", lambda e: e.tensor_tensor(out=B["xw"].rearrange("p (h q) -> p h q", h=8), in0=xtmv[:, c, :].rearrange("p (h q) -> p h q", h=8),
                                                    in1=B["Etv"][:, :, lastc:lastc + 1].broadcast_to([128, 8, 64]), op=ALU.mult), reads=[f"xtm{c}", f"Et{si}"], writes=[f"xw{si}"])

            def back(ui):
                c, d = units[ui]
                si, pa = ui % NS, ui % 2
                B = sets[si]
                pY, pS = 4 + pa, 6 + pa
                firstv[ui] = c not in visited
                visited.add(c)
                keep = (keepf if d == 0 else keepb)[:, c:c + 1]
                op("dve", lambda e: e.tensor_scalar(out=B["Hin"], in0=H[d], scalar1=keep, scalar2=None, op0=ALU.mult), reads=[f"H{d}", "pcs"], writes=[f"Hin{si}"])
                for h in range(8):
                    o = ps[pY][:, h * 64:(h + 1) * 64]
                    op("pe", lambda e, h=h, o=o: e.matmul(o, lhsT=B["MTv"][:, h, :], rhs=xtmv[:, c, h * 64:(h + 1) * 64], start=True, stop=False),
                       reads=[f"MT{si}", f"xtm{c}"], writes=[PS[pY]], inc=False)
                    second = not firstv[ui]
                    op("pe", lambda e, h=h, o=o: e.matmul(o, lhsT=B["CEv"][:, h, :], rhs=B["Hin"][:, h * 64:(h + 1) * 64], start=False, stop=(d == 1 and not second)),
                       reads=[f"CE{si}", f"Hin{si}"], writes=[PS[pY]], inc=(d == 1 and not second))
                    if d == 0:
                        op("pe", lambda e, h=h, o=o: e.matmul(o, lhsT=dDv[:, h, :], rhs=xtmv[:, c, h * 64:(h + 1) * 64], start=False, stop=(not second)),
                           reads=["dD", f"xtm{c}"], writes=[PS[pY]], inc=(not second))
                    if second:
                        op("pe", lambda e, h=h, o=o: e.matmul(o, lhsT=id16, rhs=yaccv[:, c, h * 64:(h + 1) * 64], start=False, stop=True),
                           reads=["id16", f"yacc{c}"], writes=[PS[pY]], inc=True)
                op("pe", lambda e: e.matmul(ps[pS], lhsT=Btmv[:, c, :], rhs=B["xw"], start=True, stop=True), reads=[f"Btm{c}", f"xw{si}"], writes=[PS[pS]], inc=True)
                op("dve", lambda e: e.tensor_tensor(out=H[d].rearrange("p (h q) -> p h q", h=8), in0=H[d].rearrange("p (h q) -> p h q", h=8),
                                                    in1=B["cd"].unsqueeze(2).broadcast_to([128, 8, 64]), op=ALU.mult), reads=[f"H{d}", f"cd{si}"], writes=[f"H{d}"])
                op("dve", lambda e: e.tensor_tensor(out=H[d], in0=H[d], in1=ps[pS], op=ALU.add), reads=[f"H{d}", PS[pS]], writes=[f"H{d}"])

            def backB(ui):
                c, d = units[ui]
                si, pa = ui % NS, ui % 2
                B = sets[si]
                pY, pS = 4 + pa, 6 + pa
                first = firstv[ui]
                while pend:
                    hs_i, d2, sidx2 = pend.pop(0)
                    for b in range(4):
                        op("pe", lambda e, b=b: e.transpose(ps[pS][:, b * 128:(b + 1) * 128], Hs[hs_i][:, b * 128:(b + 1) * 128], id32), reads=[f"Hs{hs_i}", "id32"], writes=[PS[pS]])
                    op("act", lambda e: e.copy(so, ps[pS]), reads=[PS[pS]], writes=["so"])
                    dma("sp", dso, lambda e: e.dma_start(out=st_d[d2][sidx2, g * 512:(g + 1) * 512, :].rearrange("(b p) n -> p b n", p=128),
                                                        in_=so.rearrange("p (b n) -> p b n", b=4)), reads=["so"], writes=[f"st{d2}_{sidx2}_{g}"])
                if (d == 0 and c % 2 == 1) or (d == 1 and c % 2 == 0):
                    hs_i = nst[0] % 2
                    nst[0] += 1
                    op("act", lambda e: e.copy(Hs[hs_i], H[d]), reads=[f"H{d}"], writes=[f"Hs{hs_i}"])
                    pend.append((hs_i, d, c // 2))
                if first:
                    op("act", lambda e: e.copy(yaccv[:, c, :], ps[pY]), reads=[PS[pY]], writes=[f"yacc{c}"])
                else:
                    s = nfin[0] % 2
                    nfin[0] += 1
                    yt = ytb[s]
                    op("dve", lambda e: e.tensor_tensor(out=ygo[s], in0=ps[pY], in1=szAv[:, c, :], op=ALU.mult), reads=[PS[pY], f"sz{c}"], writes=[f"ygo{s}"])
                    op("act", lambda e: e.activation(out=sqs, in_=ygo[s], func=AF.Square, accum_out=ssy[:, c * 4 + g:c * 4 + g + 1]), reads=[f"ygo{s}"], writes=["sqs", f"ssy{c}"])
                    dma("sp", dyg[s], lambda e: e.dma_start(out=yg_d[c * 128:(c + 1) * 128, g * 512:(g + 1) * 512], in_=ygo[s]),
                        reads=[f"ygo{s}"], writes=[f"yg_d{c}"])

            frontPE(0)
            for t_ in range(len(units) + 2):
                if t_ + 1 < len(units):
                    frontPE(t_ + 1)
                if t_ >= 2:
                    back(t_ - 2)
                if t_ < len(units):
                    front(t_)
                if t_ >= 2:
                    backB(t_ - 2)
            while pend:
                hs_i, d2, sidx2 = pend.pop(0)
                for b in range(4):
                    op("pe", lambda e, b=b: e.transpose(ps[6][:, b * 128:(b + 1) * 128], Hs[hs_i][:, b * 128:(b + 1) * 128], id32), reads=[f"Hs{hs_i}", "id32"], writes=[PS[6]])
                op("act", lambda e: e.copy(so, ps[6]), reads=[PS[6]], writes=["so"])
                dma("sp", dso, lambda e: e.dma_start(out=st_d[d2][sidx2, g * 512:(g + 1) * 512, :].rearrange("(b p) n -> p b n", p=128),
                                                    in_=so.rearrange("p (b n) -> p b n", b=4)), reads=["so"], writes=[f"st{d2}_{sidx2}_{g}"])

        if stop_after == "B":
            S.barrier()
            dtap("ssy", ssy, "ssy", F32)
            S.final_wait("sp")
            S.emit(nc, st)
            return nc
        S.barrier()
        A.off = P_MARK
        rmode[0] = "pool"
        wbs = A.alloc(16 * 1024, BF16)
        wbsv = wbs.rearrange("p (k n) -> p k n", k=16)
        wga = A.alloc(8 * 1024, BF16)
        wgav = wga.rearrange("p (k n) -> p k n", k=8)
        wout = A.alloc(8 * 1024, BF16)
        woutv = wout.rearrange("p (k n) -> p k n", k=8)
        ygt = [A.alloc(2048, BF16) for _ in range(3)]
        msl = [A.alloc(1024, BF16) for _ in range(3)]
        xl = [A.alloc(1024, F32) for _ in range(3)]
        yTc = [A.alloc(16 * 128, BF16) for _ in range(2)]
        sga = [A.alloc(1024, F32) for _ in range(2)]
        t1 = [A.alloc(1024, F32) for _ in range(2)]
        mg = [A.alloc(1024, BF16) for _ in range(2)]
        mgT = [A.alloc(1024, BF16) for _ in range(2)]
        x1 = [A.alloc(1024, F32) for _ in range(3)]
        scr = [A.alloc(1024, F32) for _ in range(3)]
        rsy = A.alloc(NCH, F32)
        tnc = A.alloc(NCH * 4, F32)
        dwc = dsem()
        dwc2, dwc3 = dsem(), dsem()
        dma("pool", dwc, lambda e: e.dma_start(out=wgav, in_=wga_d.rearrange("(k p) n -> p k n", p=128)), writes=["wga"])
        dma("pool", dwc2, lambda e: e.dma_start(out=wbsv[:, 0:8, :], in_=wbs_d[0:1024, :].rearrange("(k p) n -> p k n", p=128)), writes=["wbs"])
        dma("pool", dwc2, lambda e: e.dma_start(out=wbsv[:, 8:16, :], in_=wbs_d[1024:2048, :].rearrange("(k p) n -> p k n", p=128)), writes=["wbs"])
        dma("pool", dwc3, lambda e: e.dma_start(out=woutv, in_=wout_d.rearrange("(k p) n -> p k n", p=128)), writes=["wout"])
        for kb in range(16):
            op("dve", lambda e, kb=kb: e.tensor_scalar(out=wbsv[:, kb, :], in0=wbsv[:, kb, :], scalar1=featc[:, R_SSDN + kb:R_SSDN + kb + 1], scalar2=None, op0=ALU.mult),
               reads=["wbs", "featc"], writes=["wbs"])
        op("dve", lambda e: e.tensor_reduce(out=rsy, in_=ssy.rearrange("p (c g) -> p c g", g=4), axis=mybir.AxisListType.X, op=ALU.add), reads=["ssy"], writes=["rsy"])
        rstd_from_ss(rsy, 2048, rsy, "rsy")
        dl = [dsem(), dsem(), dsem()]
        dx1 = [dsem(), dsem(), dsem()]

        def c_load(c):
            s3 = c % 3
            dma("sp", dl[s3], lambda e: e.dma_start(out=ygt[s3], in_=yg_d[c * 128:(c + 1) * 128, :]), reads=[f"yg_d{c}"], writes=[f"ygt{s3}"])
            dma("sp", dl[s3], lambda e: e.dma_start(out=msl[s3], in_=msgu_d[c * 128:(c + 1) * 128, :]), reads=[f"msgu_d{c}"], writes=[f"msl{s3}"])
            dma("sp", dl[s3], lambda e: e.dma_start(out=xl[s3], in_=xin[c * 128:(c + 1) * 128, :]), writes=[f"xl{s3}"])

        def c_front(c):
            s = c % 2
            s3 = c % 3
            ct = slice(c * 128, (c + 1) * 128)
            yTcv = yTc[s].rearrange("p (k t) -> p k t", k=16)
            for nb in range(2):
                for k in range(8):
                    op("pe", lambda e, nb=nb, k=k: e.matmul(ps[2 + nb], lhsT=hTv[:, k, ct], rhs=wgav[:, k, nb * 512:(nb + 1) * 512],
                                                            start=(k == 0), stop=(k == 7)), reads=[hk(c), "wga"], writes=[PS[2 + nb]], inc=(k == 7))
                sl = slice(nb * 512, (nb + 1) * 512)
                op("act", lambda e, nb=nb, sl=sl: e.activation(out=sga[s][:, sl], in_=ps[2 + nb], func=AF.Sigmoid), reads=[PS[2 + nb]], writes=[f"sga{s}"])
            for kb in range(16):
                pp = kb // 8
                op("pe", lambda e, kb=kb, pp=pp: e.transpose(psb[pp][:, (kb % 8) * 128:(kb % 8 + 1) * 128], ygt[s3][:, kb * 128:(kb + 1) * 128], id16),
                   reads=[f"ygt{s3}", "id16"], writes=[PS[pp]])
            op("act", lambda e: e.copy(yTc[s][:, 0:1024], psb[0]), reads=[PS[0]], writes=[f"yTc{s}a"])
            op("dve", lambda e: e.tensor_copy(yTc[s][:, 1024:2048], psb[1]), reads=[PS[1]], writes=[f"yTc{s}b"])
            for nb in range(2):
                for kb in range(16):
                    op("pe", lambda e, nb=nb, kb=kb: e.matmul(ps[nb], lhsT=yTcv[:, kb, :], rhs=wbsv[:, kb, nb * 512:(nb + 1) * 512],
                                                              start=(kb == 0), stop=(kb == 15)), reads=[f"yTc{s}" + ("a" if kb < 8 else "b"), "wbs"], writes=[PS[nb]], inc=(kb == 15))
                sl = slice(nb * 512, (nb + 1) * 512)
                op("dve", lambda e, nb=nb, sl=sl: e.scalar_tensor_tensor(out=t1[s][:, sl], in0=ps[nb], scalar=rsy[:, c:c + 1], in1=sga[s][:, sl],
                                                                          op0=ALU.mult, op1=ALU.mult), reads=[PS[nb], "rsy", f"sga{s}"], writes=[f"t1{s}"])
            op("dve", lambda e: e.tensor_tensor(out=mg[s], in0=t1[s], in1=msl[s3], op=ALU.add), reads=[f"t1{s}", f"msl{s3}"], writes=[f"mg{s}"])

        def c_tail1(c):
            s = c % 2
            s3 = c % 3
            q = c % 3
            mgTv = mgT[s].rearrange("p (k t) -> p k t", k=8)
            tk = f"tnc{c}"
            for k in range(8):
                op("pe", lambda e, k=k: e.transpose(psb[4][:, k * 128:(k + 1) * 128], mg[s][:, k * 128:(k + 1) * 128], id16), reads=[f"mg{s}", "id16"], writes=[PS[4]])
            op("act", lambda e: e.copy(mgT[s], psb[4]), reads=[PS[4]], writes=[f"mgT{s}"])

        def c_tail1b(c):
            s = c % 2
            s3 = c % 3
            q = c % 3
            mgTv = mgT[s].rearrange("p (k t) -> p k t", k=8)
            tk = f"tnc{c}"
            for nb in range(2):
                for k in range(8):
                    op("pe", lambda e, nb=nb, k=k: e.matmul(ps[5 + nb], lhsT=mgTv[:, k, :], rhs=woutv[:, k, nb * 512:(nb + 1) * 512],
                                                            start=(k == 0), stop=(k == 7)), reads=[f"mgT{s}", "wout"], writes=[PS[5 + nb]], inc=(k == 7))
                op("act", lambda e, nb=nb: e.activation(out=scr[q][:, nb * 512:(nb + 1) * 512], in_=ps[5 + nb], func=AF.Square, accum_out=tnc[:, c * 4 + nb:c * 4 + nb + 1]),
                   reads=[PS[5 + nb]], writes=[f"scr{q}", tk])
            rc = tnc[:, c * 4 + 2:c * 4 + 3]
            op("dve", lambda e: e.tensor_tensor(out=rc, in0=tnc[:, c * 4:c * 4 + 1], in1=tnc[:, c * 4 + 1:c * 4 + 2], op=ALU.add), reads=[tk], writes=[tk])
            rstd_from_ss(rc, 1024, rc, tk)
            for nb in range(2):
                sl = slice(nb * 512, (nb + 1) * 512)
                op("dve", lambda e, nb=nb, sl=sl: e.scalar_tensor_tensor(out=t1[s][:, sl], in0=ps[5 + nb], scalar=rc, in1=gw1[:, sl],
                                                                          op0=ALU.mult, op1=ALU.mult), reads=[PS[5 + nb], tk, "g1"], writes=[f"t1{s}"])
            op("dve", lambda e: e.tensor_tensor(out=x1[q], in0=t1[s], in1=xl[s3], op=ALU.add), reads=[f"t1{s}", f"xl{s3}"], writes=[f"x1{q}"])
            dma("pool", dx1[q], lambda e: e.dma_start(out=yout[c * 128:(c + 1) * 128, :], in_=x1[q]), reads=[f"x1{q}"], writes=[f"yout{c}"])

        def c_tail2a(c):
            s = c % 3
            tk = f"tnc{c}"
            r2 = tnc[:, c * 4 + 3:c * 4 + 4]
            op("act", lambda e: e.activation(out=scr[s], in_=x1[s], func=AF.Square, accum_out=r2), reads=[f"x1{s}"], writes=[f"scr{s}", tk])
            rstd_from_ss(r2, 1024, r2, tk)
            op("dve", lambda e: e.tensor_scalar(out=scr[s], in0=x1[s], scalar1=r2, scalar2=None, op0=ALU.mult), reads=[f"x1{s}", tk], writes=[f"scr{s}"])

        def c_tail2b(c):
            s = c % 3
            for half in range(2):
                pb_ = 7 if half == 0 else 6
                for k4 in range(4):
                    k = half * 4 + k4
                    op("pe", lambda e, k=k, k4=k4, pb_=pb_: e.transpose(ps[pb_][:, k4 * 128:(k4 + 1) * 128], scr[s][:, k * 128:(k + 1) * 128], id32),
                       reads=[f"scr{s}", "id32"], writes=[PS[pb_]])
                for k4 in range(4):
                    k = half * 4 + k4
                    if k4 % 2 == 0:
                        op("act", lambda e, k=k, k4=k4, pb_=pb_: e.activation(out=hTv[:, k, c * 128:(c + 1) * 128], in_=ps[pb_][:, k4 * 128:(k4 + 1) * 128],
                                                                            func=AF.Identity, scale=g2[:, k:k + 1], bias=modT[:, 16 + k:16 + k + 1]),
                           reads=[PS[pb_], "g12", "modT"], writes=[hk(c)])
                    else:
                        op("dve", lambda e, k=k, k4=k4, pb_=pb_: e.tensor_scalar(out=hTv[:, k, c * 128:(c + 1) * 128], in0=ps[pb_][:, k4 * 128:(k4 + 1) * 128],
                                                                                scalar1=g2[:, k:k + 1], scalar2=modT[:, 16 + k:16 + k + 1], op0=ALU.mult, op1=ALU.add),
                           reads=[PS[pb_], "g12", "modT"], writes=[hk(c)])

        c_load(0)
        c_load(1)
        c_front(0)
        for c in range(NCH + 2):
            if c + 2 < NCH:
                c_load(c + 2)
            if c >= 2:
                c_tail2a(c - 2)
            if c + 1 < NCH:
                c_front(c + 1)
            if c < NCH:
                c_tail1(c)
            if c >= 2:
                c_tail2b(c - 2)
            if c < NCH:
                c_tail1b(c)

        if stop_after == "C":
            dtap("hT", hT, HT_ALL, BF16)
            S.final_wait("sp")
            S.emit(nc, st)
            return nc
        S.barrier()
        A.off = P_MARK
        actT = A.alloc(NFB * NT, BF16)
        actTv = actT.rearrange("p (i t) -> p i t", i=NFB)
        F_MARK = A.off
        SW = 65 + NT + 65
        wu = [A.alloc(8 * 128, BF16) for _ in range(2)]
        fstg = [A.alloc(SW, BF16) for _ in range(2)]
        dgf = [A.alloc(9 * 128, BF16) for _ in range(2)]
        xfl = [A.alloc(3 * 32, BF16) for _ in range(2)]
        xfr = [A.alloc(3 * 32, BF16) for _ in range(2)]
        gbuf = A.alloc(NT, BF16)
        dwu = [dsem(), dsem()]
        for s in range(2):
            op("pool", lambda e, s=s: e.memset(fstg[s], 0.0), writes=[f"fstg{s}"])
        A.top = A.nbytes
        wd = A.alloc_top(NFB * 1024, BF16)
        wdv = wd.rearrange("p (i n) -> p i n", i=NFB)
        assert A.top >= A.off, (A.top, A.off)
        dwd2 = dsem()
        it = 0
        for i in range(NFB):
            if i == NFB - 4:
                for i0 in range(0, NFB, 11):
                    dma("pool", dwd2, lambda e, i0=i0: e.dma_start(out=wdv[:, i0:i0 + 11, :], in_=wdn_d[i0 * 128:(i0 + 11) * 128, :].rearrange("(i p) n -> p i n", p=128)), writes=["wd"])
            for part in range(2):
                blk = part * NFB + i
                s = it % 2
                it += 1
                wuv = wu[s].rearrange("p (k n) -> p k n", k=8)
                dgv2 = dgf[s].rearrange("p (t i) -> p t i", t=9)
                dma("pool", dwu[s], lambda e, blk=blk, wuv=wuv: e.dma_start(out=wuv, in_=wup_d[:, blk * 128:(blk + 1) * 128].rearrange("(k p) n -> p k n", p=128)),
                    writes=[f"wu{s}"])
                fcb = R_FCONV + blk * 10
                op("dve", lambda e, dgv2=dgv2, fcb=fcb: e.tensor_tensor(out=dgv2, in0=id32.unsqueeze(1).broadcast_to([128, 9, 128]),
                                                                        in1=featc[:, fcb:fcb + 9].unsqueeze(2).broadcast_to([128, 9, 128]), op=ALU.mult),
                   reads=["id32", "featc"], writes=[f"dgf{s}"])
                for t in range(4):
                    ts = slice(t * 512, (t + 1) * 512)
                    for k in range(8):
                        op("pe", lambda e, t=t, k=k, wuv=wuv, ts=ts: e.matmul(ps[t], lhsT=wuv[:, k, :], rhs=hTv[:, k, ts], start=(k == 0), stop=(k == 7)),
                           reads=HT_ALL[4 * t:4 * t + 4] + [f"wu{s}"], writes=[PS[t]], inc=(k == 7))
                    if t % 2 == 0:
                        op("act", lambda e, t=t, s=s: e.copy(fstg[s][:, 65 + t * 512:65 + (t + 1) * 512], ps[t]), reads=[PS[t]], writes=[f"fstg{s}"])
                    else:
                        op("dve", lambda e, t=t, s=s: e.tensor_copy(fstg[s][:, 65 + t * 512:65 + (t + 1) * 512], ps[t]), reads=[PS[t]], writes=[f"fstg{s}"])
                xflv = xfl[s].rearrange("p (r m) -> p r m", r=3)
                xfrv = xfr[s].rearrange("p (r m) -> p r m", r=3)
                for dr in (-1, 0, 1):
                    op("pool", lambda e, dr=dr, s=s, xflv=xflv: e.tensor_tensor(out=xflv[:, dr + 1, :], in0=fstg[s][:, 64 * dr + 64:64 * dr + 64 + 1985:64],
                                                                                 in1=nff[:, 0:32], op=ALU.mult), reads=[f"fstg{s}", "pcs"], writes=[f"xfl{s}"])
                    op("pool", lambda e, dr=dr, s=s, xfrv=xfrv: e.tensor_tensor(out=xfrv[:, dr + 1, :], in0=fstg[s][:, 64 * dr + 129:64 * dr + 129 + 1985:64],
                                                                                 in1=nff[:, 1:33], op=ALU.mult), reads=[f"fstg{s}", "pcs"], writes=[f"xfr{s}"])
                for t in range(4):
                    pp = 4 + t
                    n = 0
                    for dr in (-1, 0, 1):
                        for dcc in (-1, 0, 1):
                            tap = (dr + 1) * 3 + dcc + 1
                            o0 = 65 + t * 512 + 64 * dr + dcc
                            op("pe", lambda e, tap=tap, o0=o0, pp=pp, s=s, dgv2=dgv2, n=n: e.matmul(ps[pp], lhsT=dgv2[:, tap, :], rhs=fstg[s][:, o0:o0 + 512],
                                                                                                     start=(n == 0), stop=False), reads=[f"dgf{s}", f"fstg{s}"], writes=[PS[pp]], inc=False)
                            n += 1
                    for dr in (-1, 0, 1):
                        op("pe", lambda e, dr=dr, pp=pp, t=t, dgv2=dgv2, xflv=xflv: e.matmul(ps[pp][:, 0:512:64], lhsT=dgv2[:, (dr + 1) * 3, :], rhs=xflv[:, dr + 1, 8 * t:8 * t + 8],
                                                                                              start=False, stop=False), reads=[f"dgf{s}", f"xfl{s}"], writes=[PS[pp]], inc=False)
                        op("pe", lambda e, dr=dr, pp=pp, t=t, dgv2=dgv2, xfrv=xfrv: e.matmul(ps[pp][:, 63:512:64], lhsT=dgv2[:, (dr + 1) * 3 + 2, :], rhs=xfrv[:, dr + 1, 8 * t:8 * t + 8],
                                                                                              start=False, stop=(dr == 1)), reads=[f"dgf{s}", f"xfr{s}"], writes=[PS[pp]], inc=(dr == 1))
                    ts = slice(t * 512, (t + 1) * 512)
                    if part == 0:
                        op("act", lambda e, pp=pp, ts=ts, fcb=fcb: e.activation(out=gbuf[:, ts], in_=ps[pp], func=AF.Gelu_apprx_tanh, bias=featc[:, fcb + 9:fcb + 10]),
                           reads=[PS[pp], "featc"], writes=["gbuf"])
                    else:
                        op("dve", lambda e, pp=pp, ts=ts, fcb=fcb, i=i: e.scalar_tensor_tensor(out=actTv[:, i, ts], in0=ps[pp], scalar=featc[:, fcb + 9:fcb + 10], in1=gbuf[:, ts],
                                                                                              op0=ALU.add, op1=ALU.mult), reads=[PS[pp], "featc", "gbuf"], writes=["actT"])

        if stop_after == "2":
            dtap("actT", actT, "actT", BF16)
            S.final_wait("sp")
            S.emit(nc, st)
            return nc
        S.barrier(exclude=(dwd2,))
        A.off = F_MARK
        x1l = [A.alloc(1024, F32) for _ in range(2)]
        yo = [A.alloc(1024, F32) for _ in range(2)]
        dl2 = [dsem(), dsem()]
        dy = [dsem(), dsem()]
        for c in range(NCH):
            s = c % 2
            ct = slice(c * 128, (c + 1) * 128)
            dma("sp", dl2[s], lambda e, s=s, c=c: e.dma_start(out=x1l[s], in_=yout[c * 128:(c + 1) * 128, :]), reads=[f"yout{c}"], writes=[f"x1l{s}"])
            pb = 2 * s
            for nb in range(2):
                for i in range(NFB):
                    op("pe", lambda e, nb=nb, i=i, pb=pb: e.matmul(ps[pb + nb], lhsT=actTv[:, i, ct], rhs=wdv[:, i, nb * 512:(nb + 1) * 512],
                                                                  start=(i == 0), stop=(i == NFB - 1)), reads=["actT", "wd"], writes=[PS[pb + nb]], inc=(i == NFB - 1))
                op("act", lambda e, nb=nb, pb=pb, s=s: e.activation(out=yo[s][:, nb * 512:(nb + 1) * 512], in_=ps[pb + nb], func=AF.Square, accum_out=tiny[:, 8 + nb:9 + nb]),
                   reads=[PS[pb + nb]], writes=[f"yo{s}", "tiny"])
            op("dve", lambda e: e.tensor_tensor(out=tiny[:, 10:11], in0=tiny[:, 8:9], in1=tiny[:, 9:10], op=ALU.add), reads=["tiny"], writes=["tiny"])
            rstd_from_ss(tiny[:, 10:11], 1024, tiny[:, 10:11], "tiny")
            for nb in range(2):
                sl = slice(nb * 512, (nb + 1) * 512)
                op("dve", lambda e, nb=nb, sl=sl, pb=pb, s=s: e.scalar_tensor_tensor(out=yo[s][:, sl], in0=ps[pb + nb], scalar=tiny[:, 10:11], in1=gw2[:, sl],
                                                                                  op0=ALU.mult, op1=ALU.mult), reads=[PS[pb + nb], "tiny", "g2"], writes=[f"yo{s}"])
            op("pool", lambda e, s=s: e.tensor_tensor(out=yo[s], in0=yo[s], in1=x1l[s], op=ALU.add), reads=[f"yo{s}", f"x1l{s}"], writes=[f"yo{s}"])
            dma("pool", dy[s], lambda e, s=s, c=c: e.dma_start(out=yout[c * 128:(c + 1) * 128, :], in_=yo[s]), reads=[f"yo{s}"], writes=[f"yout{c}"])

        S.final_wait("sp")
        S.emit(nc, st)
    return nc


def _prep_inputs(I):
    f = np.float32
    xp, xsm = I["x_prompt"], I["x_sample"]
    w_in = I["w_in"][0]
    Z, XBC = 2048, 2048 + 3072
    DT = XBC + 64
    U, V, GA = DT + 1024, DT + 2048, DT + 3072
    z_c = w_in[:, 0:Z]
    xbc_c = w_in[:, Z:XBC]
    wdt = np.ascontiguousarray(w_in[:, XBC:DT])
    u_c, v_c, ga_c, gb_c = w_in[:, DT:U], w_in[:, U:V], w_in[:, V:GA], w_in[:, GA:GA + 1024]
    wA = np.ascontiguousarray(np.concatenate([u_c, v_c, gb_c], axis=1))
    wB = np.stack([np.concatenate([z_c[:, g * 512:(g + 1) * 512], xbc_c[:, g * 512:(g + 1) * 512],
                                   xbc_c[:, 2048 + g * 128:2048 + (g + 1) * 128], xbc_c[:, 2560 + g * 128:2560 + (g + 1) * 128]], axis=1)
                   for g in range(4)]).astype(f)
    wmod_full = I["w_mod"][0]
    wmod = np.ascontiguousarray(wmod_full.reshape(1024, 6, 1024).transpose(1, 0, 2))
    bmod = I["b_mod"][0].reshape(6, 1024)
    cw, cb = I["ssd_conv_w"][0], I["ssd_conv_b"][0]
    fw, fbias = I["ffn_conv_w"][0].reshape(9, 5632), I["ffn_conv_b"][0]
    fw_prompt = np.zeros_like(fw)
    fw_prompt[3:6] = fw[3:6]
    shared = dict(
        wmod=wmod, wA=wA, wB=wB, wdt=wdt, wga=np.ascontiguousarray(ga_c),
        wsT=np.ascontiguousarray(I["sgu_w"][0].transpose(2, 0, 1)), sgub=np.ascontiguousarray(I["sgu_b"][0].reshape(1, 1024)),
        wbg=I["w_branch_sgu"][0], wbs=I["w_branch_ssd"][0], wout=I["w_out"][0], wup=I["ffn_w_up"][0], wdn=I["ffn_w_down"][0],
    )
    rowb = np.zeros((1, NRB), f)
    rowb[0, B_NMPOST:B_NMPOST + 1024] = I["norm_mix_post"][0]
    rowb[0, B_NFPOST:B_NFPOST + 1024] = I["norm_ffn_post"][0]
    rowb[0, B_G1:B_G1 + 1024] = bmod[2]
    rowb[0, B_G2:B_G2 + 1024] = bmod[5]
    rowb[0, B_SGW:B_SGW + 1024] = I["sgu_norm_w"][0]
    rowb[0, B_SGB:B_SGB + 1024] = I["sgu_norm_b"][0]
    rowb[0, B_SSDN:B_SSDN + 2048] = I["ssd_norm"][0]
    rowb[0, B_SSDD:B_SSDD + 32] = I["ssd_d"][0]

    def rows_for(cvec, fwc):
        r = np.zeros((NROWS, 128), f)
        r[R_C:R_C + 8] = cvec.reshape(8, 128)
        r[R_NMP:R_NMP + 8] = I["norm_mix_pre"][0].reshape(8, 128)
        r[R_NFP:R_NFP + 8] = I["norm_ffn_pre"][0].reshape(8, 128)
        r[R_SH1:R_SH1 + 8] = bmod[0].reshape(8, 128)
        r[R_SC1:R_SC1 + 8] = bmod[1].reshape(8, 128)
        r[R_SH2:R_SH2 + 8] = bmod[3].reshape(8, 128)
        r[R_SC2:R_SC2 + 8] = bmod[4].reshape(8, 128)
        r[R_DTB, 0:64] = I["ssd_dt_bias"][0].reshape(64)
        r[R_ALOG, 0:64] = I["ssd_a_log"][0].reshape(64)
        r[R_SSDN:R_SSDN + 16] = I["ssd_norm"][0].reshape(16, 128)
        for g in range(4):
            chans = [slice(g * 512 + b * 128, g * 512 + (b + 1) * 128) for b in range(4)] + \
                    [slice(2048 + g * 128, 2048 + (g + 1) * 128), slice(2560 + g * 128, 2560 + (g + 1) * 128)]
            for b, sl in enumerate(chans):
                base = R_SCONV + (g * 6 + b) * 4
                r[base:base + 3] = cw[:, sl]
                r[base + 3] = cb[sl]
        for b in range(44):
            base = R_FCONV + b * 10
            r[base:base + 9] = fwc[:, b * 128:(b + 1) * 128]
            r[base + 9] = fbias[b * 128:(b + 1) * 128]
        return r

    maps = []
    for core in range(8):
        pc = np.zeros((128, 66), f)
        if core < 4:
            b = core
            x = xsm[b]
            cvec = I["c"][b]
            h0 = np.stack([I["state_ssd_fwd"][b, 0].reshape(2048, 128), I["state_ssd_bwd"][b, 0].reshape(2048, 128)])
            pc[:, 0:32] = 1.0
            pc[:, 32] = 0.0
            pc[:, 33:66] = -1.0
            fwc = fw
        else:
            q = core - 4
            x = xp[8 * q:8 * q + 8].reshape(NT, D)
            cvec = I["c_ctx"]
            h0 = np.zeros((2, 2048, 128), f)
            pc[:, 0:16] = np.tile(np.array([0.0, 1.0], f), 8)[None, :]
            pc[:, 16:32] = np.tile(np.array([1.0, 0.0], f), 8)[None, :]
            pc[:, 32] = -1.0
            nffv = np.zeros(33, f)
            nffv[0::4] = -1.0
            pc[:, 33:66] = nffv[None, :]
            fwc = fw_prompt
        m = dict(shared)
        m.update(xin=np.ascontiguousarray(x, dtype=f), rows=rows_for(cvec, fwc), rowb=rowb, pc=pc, h0=np.ascontiguousarray(h0, dtype=f))
        maps.append(m)
    return maps


_NC_CACHE = {}


def kernel(**inputs):
    I = {k: np.asarray(v) for k, v in inputs.items()}
    maps = _prep_inputs(I)
    if "nc" not in _NC_CACHE:
        _NC_CACHE["nc"] = build_program()
    nc = _NC_CACHE["nc"]
    res = run_bass_kernel_spmd(nc, maps, core_ids=list(range(8)))
    R = res.results
    y_sample = np.stack([R[b]["yout"] for b in range(4)]).astype(np.float32)
    y_prompt = np.concatenate([R[4 + q]["yout"].reshape(8, 256, D) for q in range(4)], axis=0).astype(np.float32)
    nsf = np.concatenate([R[4 + q]["stf"].reshape(8, 1, 32, 64, 128) for q in range(4)], axis=0).astype(np.float32)
    nsb = np.concatenate([R[4 + q]["stb"].reshape(8, 1, 32, 64, 128) for q in range(4)], axis=0).astype(np.float32)
    return (y_prompt, y_sample, nsf, nsb)
```

```python
import types
import numpy as np
from contextlib import ExitStack
import concourse.bass as bass
import concourse.mybir as mybir
from concourse.bass_utils import run_bass_kernel_spmd

F32 = mybir.dt.float32
BF16 = mybir.dt.bfloat16
AF = mybir.ActivationFunctionType
ALU = mybir.AluOpType

ENGS = ("pe", "act", "dve", "pool", "sp")
D = 1024
NT = 2048
NCH = 16
EPS = 1e-6
DFF = 2816
NFB = 22
R_C, R_NMP, R_NFP, R_SH1, R_SC1, R_SH2, R_SC2, R_DTB, R_ALOG, R_SCONV, R_FCONV = 0, 8, 16, 24, 32, 40, 48, 56, 57, 58, 154
R_SSDN = 594
NROWS = 640
B_NMPOST, B_NFPOST, B_G1, B_G2, B_SGW, B_SGB, B_SSDN, B_SSDD, NRB = 0, 1024, 2048, 3072, 4096, 5120, 6144, 8192, 8224


def _freeze(fn):
    if fn is None or fn.__closure__ is None:
        return fn
    cells = []
    for c in fn.__closure__:
        try:
            cells.append(types.CellType(c.cell_contents))
        except ValueError:
            cells.append(c)
    return types.FunctionType(fn.__code__, fn.__globals__, fn.__name__, fn.__defaults__, tuple(cells))


class Sched:
    def __init__(self):
        self.q = {e: [] for e in ENGS}
        self.cnt = {e: 0 for e in ENGS}
        self.seen = {e: {} for e in ENGS}
        self.lastw = {}
        self.readers = {}
        self.dsems = set()

    def new_dma_sem(self, name):
        self.cnt[name] = 0
        self.dsems.add(name)
        return name

    def _need(self, eng, sem, val, waits):
        if val <= 0 or (eng == "pe" and sem == "pe"):
            return
        if sem in self.dsems:
            val = self.cnt[sem]
        if self.seen[eng].get(sem, 0) >= val:
            return
        self.seen[eng][sem] = val
        waits[sem] = max(waits.get(sem, 0), val)

    def _deps(self, eng, reads, writes):
        waits = {}
        for k in reads:
            w = self.lastw.get(k)
            if w is not None:
                self._need(eng, w[0], w[1], waits)
        for k in writes:
            w = self.lastw.get(k)
            if w is not None:
                self._need(eng, w[0], w[1], waits)
            for r in self.readers.get(k, ()):
                self._need(eng, r[0], r[1], waits)
        return waits

    def _record(self, ev, reads, writes):
        for k in reads:
            lst = self.readers.setdefault(k, [])
            lst.append(ev)
            if len(lst) > 48:
                d = {}
                for s, v in lst:
                    d[s] = max(d.get(s, 0), v)
                self.readers[k] = list(d.items())
        for k in writes:
            self.lastw[k] = ev
            self.readers[k] = []

    def op(self, eng, fn, reads=(), writes=(), inc=True):
        waits = self._deps(eng, reads, writes)
        if inc:
            self.cnt[eng] += 1
            ev = (eng, self.cnt[eng])
            self.q[eng].append((waits, _freeze(fn), eng, 1))
        else:
            ev = (eng, self.cnt[eng] + 1)
            self.q[eng].append((waits, _freeze(fn), eng, 0))
        self._record(ev, reads, writes)

    def dma(self, qeng, dsem, fn, reads=(), writes=()):
        waits = self._deps(qeng, reads, writes)
        self.cnt[dsem] += 16
        ev = (dsem, self.cnt[dsem])
        self.q[qeng].append((waits, _freeze(fn), dsem, 16))
        self._record(ev, reads, writes)

    def barrier(self, exclude=()):
        for e in ENGS:
            waits = {}
            for s, v in self.cnt.items():
                if s in exclude:
                    continue
                self._need(e, s, v, waits)
            if waits:
                self.q[e].append((waits, None, None, 0))

    def final_wait(self, eng="sp"):
        waits = {}
        for s, v in self.cnt.items():
            self._need(eng, s, v, waits)
        self.q[eng].append((waits, None, None, 0))

    def emit(self, nc, stack):
        sems = {s: stack.enter_context(nc.semaphore("s_" + s)) for s in self.cnt}
        block = stack.enter_context(nc.Block())
        hmap = {"pe": block.tensor, "act": block.scalar, "dve": block.vector,
                "pool": block.gpsimd, "sp": block.sync}
        for e in ENGS:
            items = self.q[e]

            def body(engh, items=items):
                for waits, fn, sem, inc in items:
                    for s, v in waits.items():
                        engh.wait_ge(sems[s], v)
                    if fn is not None:
                        ins = fn(engh)
                        if inc:
                            ins.then_inc(sems[sem], inc)
            hmap[e](body)


class Arena:
    def __init__(self, nc, stack, nbytes):
        self.t16 = stack.enter_context(nc.sbuf_tensor("arena", [128, nbytes // 2], BF16))
        self.t32 = self.t16.bitcast(F32)
        self.nbytes = nbytes
        self.off = 0

    def alloc(self, ncols, dtype, parts=128):
        sz = 2 if dtype == BF16 else 4
        nb = (ncols * sz + 63) // 64 * 64
        off = self.off
        if off + nb > getattr(self, "top", self.nbytes):
            raise RuntimeError(f"arena OOM: need {off + nb} > {getattr(self, 'top', self.nbytes)}")
        self.off += nb
        if dtype == BF16:
            return self.t16[0:parts, off // 2: off // 2 + ncols]
        return self.t32[0:parts, off // 4: off // 4 + ncols]

    def alloc_top(self, ncols, dtype, parts=128):
        sz = 2 if dtype == BF16 else 4
        nb = (ncols * sz + 63) // 64 * 64
        self.top = getattr(self, "top", self.nbytes) - nb
        off = self.top
        if dtype == BF16:
            return self.t16[0:parts, off // 2: off // 2 + ncols]
        return self.t32[0:parts, off // 4: off // 4 + ncols]


def build_program(stop_after=None):
    dbg = stop_after is not None
    nc = bass.Bass("TRN2", target_bir_lowering=False)

    def din(name, shape, dt=F32):
        return nc.dram_tensor(name, list(shape), dt, kind="ExternalInput").ap()

    xin = din("xin", [NT, D])
    rows_d = din("rows", [NROWS, 128])
    rowb_d = din("rowb", [1, NRB])
    pc_d = din("pc", [128, 66])
    h0_d = din("h0", [2, 2048, 128])
    wmod_d = din("wmod", [6, D, D])
    wA_d = din("wA", [D, 3072])
    wB_d = din("wB", [4, D, 1280])
    wdt_d = din("wdt", [D, 64])
    wga_d = din("wga", [D, D])
    wsT_d = din("wsT", [128, 8, 128])
    sgub_d = din("sgub", [1, 1024])
    wbg_d = din("wbg", [D, D])
    wbs_d = din("wbs", [2048, D])
    wout_d = din("wout", [D, D])
    wup_d = din("wup", [D, 2 * DFF])
    wdn_d = din("wdn", [DFF, D])
    yout = nc.dram_tensor("yout", [NT, D], F32, kind="ExternalOutput").ap()
    stf_d = nc.dram_tensor("stf", [8, 2048, 128], F32, kind="ExternalOutput").ap()
    stb_d = nc.dram_tensor("stb", [8, 2048, 128], F32, kind="ExternalOutput").ap()
    skind = dict(kind="ExternalOutput") if dbg else {}
    msgu_d = nc.dram_tensor("msgu_scr", [NT, D], BF16, **skind).ap()
    yg_d = nc.dram_tensor("yg_scr", [NT, 2048], BF16, **skind).ap()
    st_d = (stf_d, stb_d)

    S = Sched()
    op, dma = S.op, S.dma
    taps = {}

    def dtap(name, ap, key, dt):
        if not dbg:
            return
        shp = [ap.shape[0], int(np.prod(ap.shape[1:]))]
        dd = nc.dram_tensor("dbg_" + name, shp, dt, kind="ExternalOutput").ap()
        ds_ = S.new_dma_sem("dbg_" + name)
        dma("sp", ds_, lambda e: e.dma_start(out=dd, in_=ap), reads=[key] if isinstance(key, str) else key)

    class _Stop(Exception):
        pass

    def rowbc(off, n):
        return bass.AP(rowb_d.tensor, off, [[0, 128], [1, n]])

    with ExitStack() as st:
        A = Arena(nc, st, 200 * 1024)
        psh = [st.enter_context(nc.psum_tensor(f"ps{i}", [128, 512], F32)) for i in range(8)]
        ps = [t[:] for t in psh]
        psb = [t.bitcast(BF16)[:] for t in psh]
        PS = [f"ps{i}" for i in range(8)]
        nsem = [0]

        def dsem():
            nsem[0] += 1
            return S.new_dma_sem(f"d{nsem[0]}")

        hT = A.alloc(8 * NT, BF16)
        hTv = hT.rearrange("p (k t) -> p k t", k=8)
        featc = A.alloc(NROWS, F32)
        pcs = A.alloc(66, F32)
        id32 = A.alloc(128, F32)
        id16 = A.alloc(128, BF16)
        ones16 = A.alloc(128, BF16)
        modT = A.alloc(32, F32)
        g1 = A.alloc(8, F32)
        g2 = A.alloc(8, F32)
        gw1 = A.alloc(1024, F32)
        gw2 = A.alloc(1024, F32)
        ssy = A.alloc(64, F32)
        tiny = A.alloc(64, F32)
        mhalf = A.alloc(1, F32)
        P_MARK = A.off
        keepf, keepb, nfs, nff = pcs[:, 0:16], pcs[:, 16:32], pcs[:, 32:33], pcs[:, 33:66]

        def hk(c):
            return f"hT{c}"
        HT_ALL = [hk(c) for c in range(NCH)]

        dc = dsem()
        dma("sp", dc, lambda e: e.dma_start(out=pcs, in_=pc_d), writes=["pcs"])
        op("pool", lambda e: e.memset(id32, 0.0), writes=["id32"])
        op("pool", lambda e: e.affine_select(out=id32, in_=id32, pattern=[[-1, 128]], compare_op=ALU.not_equal,
                                             fill=1.0, base=0, channel_multiplier=1), reads=["id32"], writes=["id32"])
        op("pool", lambda e: e.tensor_copy(id16, id32), reads=["id32"], writes=["id16"])
        op("pool", lambda e: e.memset(ones16, 1.0), writes=["ones16"])
        op("pool", lambda e: e.memset(ssy, 0.0), writes=["ssy"])
        op("pool", lambda e: e.memset(mhalf, -0.5), writes=["mhalf"])
        rst = A.alloc(128, F32)
        drow = dsem()
        for b in range(NROWS // 128):
            dma("sp", drow, lambda e, b=b: e.dma_start(out=rst, in_=rows_d[b * 128:(b + 1) * 128, :]), writes=["rst"])
            op("pe", lambda e: e.transpose(ps[0][:, 0:128], rst, id32), reads=["rst", "id32"], writes=[PS[0]])
            op("dve", lambda e, b=b: e.tensor_copy(featc[:, b * 128:(b + 1) * 128], ps[0][:, 0:128]), reads=[PS[0]], writes=["featc"])
        scT = A.alloc(8, F32)
        scT16 = A.alloc(8, BF16)
        scB = A.alloc(8 * 128, BF16)
        op("act", lambda e: e.activation(out=scT, in_=featc[:, R_C:R_C + 8], func=AF.Silu), reads=["featc"], writes=["scT"])
        op("dve", lambda e: e.tensor_copy(scT16, scT), reads=["scT"], writes=["scT16"])
        for k in range(8):
            op("dve", lambda e, k=k: e.tensor_copy(scB[:, k * 128:(k + 1) * 128], scT[:, k:k + 1].broadcast_to([128, 128])),
               reads=["scT"], writes=["scB"])
        wmb = [A.alloc(8 * 1024, BF16) for _ in range(2)]
        gb_t = A.alloc(1024, F32)
        dwm = [dsem(), dsem()]
        SECS = ((0, 0), (1, 1), (2, "g1"), (3, 2), (4, 3), (5, "g2"))

        def mod_section(i):
            sec, kind = SECS[i]
            sw = i % 2
            wmv = wmb[sw].rearrange("p (k n) -> p k n", k=8)
            dma("pool", dwm[sw], lambda e: e.dma_start(out=wmv, in_=wmod_d[sec].rearrange("(k p) n -> p k n", p=128)), writes=[f"wm{sw}"])
            if isinstance(kind, int):
                for j in range(8):
                    for k in range(8):
                        op("pe", lambda e, j=j, k=k: e.matmul(ps[1][:, j:j + 1], lhsT=wmv[:, k, j * 128:(j + 1) * 128], rhs=scT16[:, k:k + 1],
                                                              start=(k == 0), stop=(k == 7)), reads=[f"wm{sw}", "scT16"], writes=[PS[1]], inc=(k == 7))
                rb = (R_SH1, R_SC1, R_SH2, R_SC2)[kind]
                op("dve", lambda e: e.tensor_tensor(out=modT[:, kind * 8:kind * 8 + 8], in0=ps[1][:, 0:8], in1=featc[:, rb:rb + 8], op=ALU.add),
                   reads=[PS[1], "featc"], writes=[f"modT{kind}"])
            else:
                boff, npo, gw = (B_G1, B_NMPOST, gw1) if kind == "g1" else (B_G2, B_NFPOST, gw2)
                for nb in range(2):
                    for k in range(8):
                        op("pe", lambda e, nb=nb, k=k: e.matmul(ps[2 + nb], lhsT=scB[:, k * 128:(k + 1) * 128], rhs=wmv[:, k, nb * 512:(nb + 1) * 512],
                                                                start=(k == 0), stop=(k == 7)), reads=[f"wm{sw}", "scB"], writes=[PS[2 + nb]], inc=(k == 7))
                dma("sp", dc, lambda e: e.dma_start(out=gb_t, in_=rowbc(boff, 1024)), writes=["gb_t"])
                dma("sp", dc, lambda e: e.dma_start(out=gw, in_=rowbc(npo, 1024)), writes=[kind])
                for nb in range(2):
                    sl = slice(nb * 512, (nb + 1) * 512)
                    op("dve", lambda e, nb=nb, sl=sl: e.tensor_tensor(out=gb_t[:, sl], in0=ps[2 + nb], in1=gb_t[:, sl], op=ALU.add),
                       reads=[PS[2 + nb], "gb_t"], writes=["gb_t"])
                op("dve", lambda e: e.tensor_tensor(out=gw, in0=gw, in1=gb_t, op=ALU.mult), reads=["gb_t", kind], writes=[kind])

        def make_g(gg, rb, so, kinds):
            op("dve", lambda e: e.tensor_scalar(out=gg, in0=modT[:, so:so + 8], scalar1=1.0, scalar2=None, op0=ALU.add),
               reads=[f"modT{kinds}"], writes=[f"g12_{so}"])
            op("dve", lambda e: e.tensor_tensor(out=gg, in0=gg, in1=featc[:, rb:rb + 8], op=ALU.mult),
               reads=[f"g12_{so}", "featc"], writes=[f"g12_{so}"])

        rmode = ["act"]

        def rstd_from_ss(ss_ap, n, out_ap, key):
            if rmode[0] == "pool":
                op("pool", lambda e: e.tensor_scalar(out=out_ap, in0=ss_ap, scalar1=1.0 / n, scalar2=EPS, op0=ALU.mult, op1=ALU.add),
                   reads=[key], writes=[key])
                op("pool", lambda e: e.tensor_tensor(out=out_ap, in0=out_ap, in1=mhalf[:, 0:1].broadcast_to(list(out_ap.shape)), op=ALU.pow),
                   reads=[key, "mhalf"], writes=[key])
            else:
                op("dve", lambda e: e.tensor_scalar(out=out_ap, in0=ss_ap, scalar1=1.0 / n, scalar2=EPS, op0=ALU.mult, op1=ALU.add),
                   reads=[key], writes=[key])
                op("act", lambda e: e.activation(out=out_ap, in_=out_ap, func=AF.Sqrt), reads=[key], writes=[key])
                op("dve", lambda e: e.reciprocal(out_ap, out_ap), reads=[key], writes=[key])

        mod_section(0)
        mod_section(1)
        make_g(g1, R_NMP, 8, 1)
        xb = [A.alloc(1024, F32) for _ in range(3)]
        xs = [A.alloc(1024, F32) for _ in range(2)]
        tn0 = A.alloc(NCH, F32)
        dx = [dsem(), dsem(), dsem()]

        def p0_P(c):
            s3, s = c % 3, c % 2
            tk = f"tn0_{c}"
            sscol = tn0[:, c:c + 1]
            dma("sp", dx[s3], lambda e: e.dma_start(out=xb[s3], in_=xin[c * 128:(c + 1) * 128, :]), writes=[f"xb{s3}"])
            op("act", lambda e: e.activation(out=xs[s], in_=xb[s3], func=AF.Square, accum_out=sscol), reads=[f"xb{s3}"], writes=[f"xs{s}", tk])
            rstd_from_ss(sscol, 1024, sscol, tk)
            op("dve", lambda e: e.tensor_scalar(out=xs[s], in0=xb[s3], scalar1=sscol, scalar2=None, op0=ALU.mult), reads=[f"xb{s3}", tk], writes=[f"xs{s}"])

        def p0_Q(c):
            s = c % 2
            pA, pB = 4 + 2 * s, 5 + 2 * s
            for k in range(8):
                pp = pA if k < 4 else pB
                op("pe", lambda e, k=k, pp=pp: e.transpose(ps[pp][:, (k % 4) * 128:(k % 4 + 1) * 128], xs[s][:, k * 128:(k + 1) * 128], id32),
                   reads=[f"xs{s}", "id32"], writes=[PS[pp]])
            for k in range(8):
                pp = pA if k < 4 else pB
                op("act", lambda e, k=k, pp=pp: e.activation(out=hTv[:, k, c * 128:(c + 1) * 128], in_=ps[pp][:, (k % 4) * 128:(k % 4 + 1) * 128],
                                                             func=AF.Identity, scale=g1[:, k:k + 1], bias=modT[:, k:k + 1]),
                   reads=[PS[pp], "g12_8", "modT0"], writes=[hk(c)])

        p0_P(0)
        nsec = 2
        for c in range(NCH):
            if c + 1 < NCH:
                p0_P(c + 1)
            p0_Q(c)
            if c % 4 == 1 and nsec < 6:
                mod_section(nsec)
                nsec += 1
        while nsec < 6:
            mod_section(nsec)
            nsec += 1
        make_g(g2, R_NFP, 24, 3)

        wA = A.alloc_top(8 * 3072, BF16)
        wAv = wA.rearrange("p (k n) -> p k n", k=8)
        wbg = A.alloc_top(8 * 1024, BF16)
        wbgv = wbg.rearrange("p (k n) -> p k n", k=8)
        wsT = A.alloc_top(8 * 128, BF16)
        wsTv = wsT.rearrange("p (g i) -> p g i", g=8)
        sgub = A.alloc_top(1024, BF16, parts=1)
        sgw = A.alloc_top(1024, F32)
        sgb = A.alloc_top(1024, F32)
        assert A.top >= A.off, (A.top, A.off)
        dwa, dwa2, dwa3 = dsem(), dsem(), dsem()
        dma("pool", dwa, lambda e: e.dma_start(out=wAv[:, :, 1024:2048], in_=wA_d[:, 1024:2048].rearrange("(k p) n -> p k n", p=128)), writes=["wAv"])
        dma("pool", dwa2, lambda e: e.dma_start(out=wAv[:, :, 0:1024], in_=wA_d[:, 0:1024].rearrange("(k p) n -> p k n", p=128)), writes=["wAu"])
        dma("pool", dwa2, lambda e: e.dma_start(out=wAv[:, :, 2048:3072], in_=wA_d[:, 2048:3072].rearrange("(k p) n -> p k n", p=128)), writes=["wAg"])
        dma("pool", dwa3, lambda e: e.dma_start(out=wbgv, in_=wbg_d.rearrange("(k p) n -> p k n", p=128)), writes=["wbg"])
        dma("pool", dwa3, lambda e: e.dma_start(out=wsTv, in_=wsT_d), writes=["wsT"])
        dma("pool", dwa3, lambda e: e.dma_start(out=sgub, in_=sgub_d), writes=["sgub"])
        dma("sp", dwa3, lambda e: e.dma_start(out=sgw, in_=rowbc(B_SGW, 1024)), writes=["sgw"])
        dma("sp", dwa3, lambda e: e.dma_start(out=sgb, in_=rowbc(B_SGB, 1024)), writes=["sgb"])

        if stop_after == "0":
            S.barrier()
            dtap("hT", hT, HT_ALL, BF16)
            dtap("modT", modT, "featc", F32)
            dtap("gw1", gw1, "g1", F32)
            S.final_wait("sp")
            S.emit(nc, st)
            return nc
        S.barrier(exclude=(dwa, dwa2, dwa3))
        A.off = P_MARK
        vh = [A.alloc(1024, F32) for _ in range(2)]
        vn = [A.alloc(1024, BF16) for _ in range(2)]
        uT = [A.alloc(1024, BF16) for _ in range(2)]
        ysT = [A.alloc(1024, BF16) for _ in range(2)]
        sgg = [A.alloc(1024, F32) for _ in range(2)]
        msg = [A.alloc(1024, BF16) for _ in range(2)]
        bstA = A.alloc(NCH * 12, F32)
        mvA = A.alloc(NCH * 2, F32)
        dms = [dsem(), dsem()]

        def a_X(c):
            s = c % 2
            ct = slice(c * 128, (c + 1) * 128)
            bst = bstA[:, c * 12:(c + 1) * 12]
            mv = mvA[:, c * 2:(c + 1) * 2]
            mk = f"mv{c}"
            for nb in range(2):
                for k in range(8):
                    op("pe", lambda e, nb=nb, k=k: e.matmul(ps[nb], lhsT=hTv[:, k, ct], rhs=wAv[:, k, 1024 + nb * 512:1024 + (nb + 1) * 512],
                                                            start=(k == 0), stop=(k == 7)), reads=[hk(c), "wAv"], writes=[PS[nb]], inc=(k == 7))
            for nb in range(2):
                op("dve", lambda e, nb=nb: e.bn_stats(bst[:, nb * 6:(nb + 1) * 6], ps[nb]), reads=[PS[nb]], writes=[mk])
            op("dve", lambda e: e.bn_aggr(mv, bst), reads=[mk], writes=[mk])
            op("dve", lambda e: e.tensor_scalar(out=mv[:, 1:2], in0=mv[:, 1:2], scalar1=EPS, scalar2=None, op0=ALU.add), reads=[mk], writes=[mk])
            op("act", lambda e: e.activation(out=mv[:, 1:2], in_=mv[:, 1:2], func=AF.Sqrt), reads=[mk], writes=[mk])
            op("dve", lambda e: e.reciprocal(mv[:, 1:2], mv[:, 1:2]), reads=[mk], writes=[mk])
            for nb in range(2):
                op("dve", lambda e, nb=nb: e.tensor_scalar(out=vh[s][:, nb * 512:(nb + 1) * 512], in0=ps[nb], scalar1=mv[:, 0:1], scalar2=mv[:, 1:2],
                                                            op0=ALU.subtract, op1=ALU.mult), reads=[PS[nb], mk], writes=[f"vh{s}"])
            for blk in range(8):
                o = ps[4 + blk // 4][:, (blk % 4) * 128:(blk % 4 + 1) * 128]
                for k in range(8):
                    op("pe", lambda e, blk=blk, k=k, o=o: e.matmul(o, lhsT=wAv[:, k, blk * 128:(blk + 1) * 128], rhs=hTv[:, k, ct],
                                                                  start=(k == 0), stop=(k == 7)), reads=[hk(c), "wAu"], writes=[PS[4 + blk // 4]], inc=(k == 7))
            for hb in range(2):
                op("act", lambda e, hb=hb: e.copy(uT[s][:, hb * 512:(hb + 1) * 512], ps[4 + hb]), reads=[PS[4 + hb]], writes=[f"uT{s}"])
            for nb in range(2):
                for k in range(8):
                    op("pe", lambda e, nb=nb, k=k: e.matmul(ps[6 + nb], lhsT=hTv[:, k, ct], rhs=wAv[:, k, 2048 + nb * 512:2048 + (nb + 1) * 512],
                                                            start=(k == 0), stop=(k == 7)), reads=[hk(c), "wAg"], writes=[PS[6 + nb]], inc=(k == 7))
                op("act", lambda e, nb=nb: e.activation(out=sgg[s][:, nb * 512:(nb + 1) * 512], in_=ps[6 + nb], func=AF.Sigmoid),
                   reads=[PS[6 + nb]], writes=[f"sgg{s}"])
            op("dve", lambda e: e.tensor_tensor(out=vh[s], in0=vh[s], in1=sgw, op=ALU.mult), reads=[f"vh{s}", "sgw"], writes=[f"vh{s}"])
            op("dve", lambda e: e.tensor_tensor(out=vn[s], in0=vh[s], in1=sgb, op=ALU.add), reads=[f"vh{s}", "sgb"], writes=[f"vn{s}"])

        def a_Y(c):
            s = c % 2
            ysTv = ysT[s].rearrange("p (g i) -> p g i", g=8)
            for g in range(8):
                o = ps[2 + g // 4][:, (g % 4) * 128:(g % 4 + 1) * 128]
                op("pe", lambda e, g=g, o=o: e.matmul(o, lhsT=vn[s][:, g * 128:(g + 1) * 128], rhs=wsTv[:, g, :], start=True, stop=False),
                   reads=[f"vn{s}", "wsT"], writes=[PS[2 + g // 4]], inc=False)
                op("pe", lambda e, g=g, o=o: e.matmul(o, lhsT=ones16[0:1, :], rhs=sgub[0:1, g * 128:(g + 1) * 128], start=False, stop=True),
                   reads=["ones16", "sgub"], writes=[PS[2 + g // 4]], inc=True)
            for hb in range(2):
                op("dve", lambda e, hb=hb: e.tensor_tensor(out=ysT[s][:, hb * 512:(hb + 1) * 512], in0=ps[2 + hb], in1=uT[s][:, hb * 512:(hb + 1) * 512], op=ALU.mult),
                   reads=[PS[2 + hb], f"uT{s}"], writes=[f"ysT{s}"])
            for nb in range(2):
                for g in range(8):
                    op("pe", lambda e, nb=nb, g=g: e.matmul(ps[2 + nb], lhsT=ysTv[:, g, :], rhs=wbgv[:, g, nb * 512:(nb + 1) * 512],
                                                            start=(g == 0), stop=(g == 7)), reads=[f"ysT{s}", "wbg"], writes=[PS[2 + nb]], inc=(g == 7))
                op("dve", lambda e, nb=nb: e.tensor_tensor(out=msg[s][:, nb * 512:(nb + 1) * 512], in0=ps[2 + nb], in1=sgg[s][:, nb * 512:(nb + 1) * 512], op=ALU.mult),
                   reads=[PS[2 + nb], f"sgg{s}"], writes=[f"msg{s}"])
            dma("sp", dms[s], lambda e: e.dma_start(out=msgu_d[c * 128:(c + 1) * 128, :], in_=msg[s]), reads=[f"msg{s}"], writes=[f"msgu_d{c}"])

        a_X(0)
        for c in range(NCH):
            if c + 1 < NCH:
                a_X(c + 1)
            a_Y(c)

        if stop_after == "A":
            S.final_wait("sp")
            S.emit(nc, st)
            return nc
        S.barrier()
        A.off = P_MARK
        A.top = A.nbytes
        hi16 = A.alloc(NT, BF16)
        lo16 = A.alloc(NT, BF16)
        btm = A.alloc(NCH * 64, F32)
        btmv = btm.rearrange("p (c r) -> p c r", c=NCH)
        e64 = A.alloc(64, BF16)
        Acol = A.alloc(1, F32)
        Db = A.alloc(32, F32)
        trif = A.alloc(128, F32)
        trib = A.alloc(128, F32)
        kbias = A.alloc(32, F32)
        B0_MARK = A.off
        wdt = A.alloc(8 * 64, BF16)
        wdtv = wdt.rearrange("p (k n) -> p k n", k=8)
        dtT = A.alloc(NT, F32)
        aT = A.alloc(NT, F32)
        acs = A.alloc(NT, F32)
        t2 = A.alloc(NT, F32)
        nstart = A.alloc(NT, F32)
        dwd = dsem()
        dma("pool", dwd, lambda e: e.dma_start(out=wdtv, in_=wdt_d.rearrange("(k p) n -> p k n", p=128)), writes=["wdt"])
        dma("sp", dc, lambda e: e.dma_start(out=Db, in_=rowbc(B_SSDD, 32)), writes=["Db"])
        op("dve", lambda e: e.tensor_scalar(out=kbias, in0=pcs[:, 0:32], scalar1=-1.0, scalar2=1.0e4, op0=ALU.add, op1=ALU.mult), reads=["pcs"], writes=["kbias"])
        op("pool", lambda e: e.memset(e64, 0.0), writes=["e64"])
        op("pool", lambda e: e.affine_select(out=e64[0:64, :], in_=e64[0:64, :], pattern=[[-1, 64]], compare_op=ALU.not_equal, fill=1.0,
                                             base=0, channel_multiplier=1), reads=["e64"], writes=["e64"])
        op("pool", lambda e: e.memset(trif, 1.0), writes=["trif"])
        op("pool", lambda e: e.affine_select(out=trif, in_=trif, pattern=[[1, 128]], compare_op=ALU.is_ge, fill=0.0, base=0, channel_multiplier=-1),
           reads=["trif"], writes=["trif"])
        op("pool", lambda e: e.memset(trib, 1.0), writes=["trib"])
        op("pool", lambda e: e.affine_select(out=trib, in_=trib, pattern=[[-1, 128]], compare_op=ALU.is_ge, fill=0.0, base=0, channel_multiplier=1),
           reads=["trib"], writes=["trib"])
        op("pool", lambda e: e.memset(nstart, 1.0), writes=["nstart"])
        op("pool", lambda e: e.memset(nstart[:, 0:NT:128], 0.0), reads=["nstart"], writes=["nstart"])
        op("act", lambda e: e.activation(out=Acol[0:64, :], in_=featc[0:64, R_ALOG:R_ALOG + 1], func=AF.Exp), reads=["featc"], writes=["Acol"])
        op("dve", lambda e: e.tensor_scalar(out=Acol[0:64, :], in0=Acol[0:64, :], scalar1=-1.0, scalar2=None, op0=ALU.mult), reads=["Acol"], writes=["Acol"])
        for t in range(4):
            ts = slice(t * 512, (t + 1) * 512)
            for k in range(8):
                op("pe", lambda e, t=t, k=k, ts=ts: e.matmul(ps[t][0:64, :], lhsT=wdtv[:, k, :], rhs=hTv[:, k, ts], start=(k == 0), stop=(k == 7)),
                   reads=HT_ALL[4 * t:4 * t + 4] + ["wdt"], writes=[PS[t]], inc=(k == 7))
            op("act", lambda e, t=t, ts=ts: e.activation(out=dtT[0:64, ts], in_=ps[t][0:64, :], func=AF.Exp, bias=featc[0:64, R_DTB:R_DTB + 1]),
               reads=[PS[t], "featc"], writes=["dtT"])
        op("act", lambda e: e.activation(out=dtT[0:64, :], in_=dtT[0:64, :], func=AF.Ln, bias=1.0), reads=["dtT"], writes=["dtT"])
        op("dve", lambda e: e.tensor_scalar(out=aT[0:64, :], in0=dtT[0:64, :], scalar1=Acol[0:64, 0:1], scalar2=None, op0=ALU.mult),
           reads=["dtT", "Acol"], writes=["aT"])
        op("dve", lambda e: e.tensor_tensor_scan(out=acs[0:64, :], data0=nstart[0:64, :], data1=aT[0:64, :], initial=0.0, op0=ALU.mult, op1=ALU.add),
           reads=["nstart", "aT"], writes=["acs"])
        acs3 = acs.rearrange("p (c i) -> p c i", c=NCH)
        t23 = t2.rearrange("p (c i) -> p c i", c=NCH)
        op("dve", lambda e: e.tensor_tensor(out=t2[32:64, :], in0=aT[32:64, :], in1=acs[32:64, :], op=ALU.subtract), reads=["aT", "acs"], writes=["t2"])
        op("dve", lambda e: e.tensor_tensor(out=t23[32:64], in0=t23[32:64], in1=acs3[32:64, :, 127:128].broadcast_to([32, NCH, 128]), op=ALU.add),
           reads=["t2", "acs"], writes=["t2"])
        op("dve", lambda e: e.tensor_copy(acs[32:64, :], t2[32:64, :]), reads=["t2"], writes=["acs"])
        op("dve", lambda e: e.tensor_copy(hi16[0:64, :], acs[0:64, :]), reads=["acs"], writes=["hi16"])
        op("dve", lambda e: e.tensor_tensor(out=lo16[0:64, :], in0=acs[0:64, :], in1=hi16[0:64, :], op=ALU.subtract), reads=["acs", "hi16"], writes=["lo16"])
        op("act", lambda e: e.activation(out=t2[0:64, :], in_=dtT[0:64, :], func=AF.Ln), reads=["dtT", "t2"], writes=["t2"])
        op("dve", lambda e: e.tensor_tensor(out=t2[0:64, :], in0=t2[0:64, :], in1=acs[0:64, :], op=ALU.subtract), reads=["t2", "acs"], writes=["t2"])
        for c in range(NCH):
            pp = 4 + c % 2
            op("pe", lambda e, c=c, pp=pp: e.transpose(ps[pp][:, 0:64], t2[0:64, c * 128:(c + 1) * 128], id32[0:64, 0:64]),
               reads=["t2", "id32"], writes=[PS[pp]])
            op("dve", lambda e, c=c, pp=pp: e.tensor_copy(btmv[:, c, :], ps[pp][:, 0:64]), reads=[PS[pp]], writes=["btm"])

        if stop_after == "B0":
            dtap("acs", acs, "acs", F32)
            dtap("dtT", dtT, "dtT", F32)
            dtap("btm", btm, "btm", F32)
            dtap("hi16", hi16, "hi16", BF16)
            S.final_wait("sp")
            S.emit(nc, st)
            return nc
        for g in range(4):
            S.barrier()
            A.off = B0_MARK
            wBz = A.alloc(8 * 512, BF16)
            wBzv = wBz.rearrange("p (k n) -> p k n", k=8)
            BT = A.alloc(NT, BF16)
            CT = A.alloc(NT, BF16)
            xtm = A.alloc(NCH * 512, BF16)
            xtmv = xtm.rearrange("p (c n) -> p c n", c=NCH)
            Btm = A.alloc(NCH * 128, BF16)
            Btmv = Btm.rearrange("p (c n) -> p c n", c=NCH)
            yacc = A.alloc(NCH * 512, BF16)
            yaccv = yacc.rearrange("p (c n) -> p c n", c=NCH)
            dD = A.alloc(8 * 128, BF16)
            dDv = dD.rearrange("p (h i) -> p h i", h=8)
            CBmA = A.alloc(NCH * 2 * 128, BF16)
            CBmAv = CBmA.rearrange("p (c d i) -> p c d i", c=NCH, d=2)
            szA = A.alloc(NCH * 512, BF16)
            szAv = szA.rearrange("p (c n) -> p c n", c=NCH)
            H = [A.alloc(512, F32), A.alloc(512, F32)]
            G_MARK = A.off
            wBx = A.alloc(8 * 768, BF16)
            wBxv = wBx.rearrange("p (k n) -> p k n", k=8)
            stg = A.alloc(6 * 2050, BF16)
            stgv = stg.rearrange("p (b t) -> p b t", b=6)
            dg = A.alloc(6 * 3 * 128, BF16)
            dgv = dg.rearrange("p (b t i) -> p b t i", b=6, t=3)
            xfL = A.alloc(48, BF16)
            xfR = A.alloc(48, BF16)
            xfLv = xfL.rearrange("p (b m) -> p b m", b=6)
            xfRv = xfR.rearrange("p (b m) -> p b m", b=6)
            xc = [A.alloc(4 * 512, BF16) for _ in range(2)]
            hst = A.alloc(512, F32)
            fb = R_SCONV + g * 24
            dwb = dsem()
            dma("pool", dwb, lambda e, g=g: e.dma_start(out=wBxv, in_=wB_d[g, :, 512:1280].rearrange("(k p) n -> p k n", p=128)), writes=["wBx"])
            dma("pool", dwb, lambda e, g=g: e.dma_start(out=wBzv, in_=wB_d[g, :, 0:512].rearrange("(k p) n -> p k n", p=128)), writes=["wBz"])
            for blk in range(6):
                for tap in range(3):
                    col = fb + blk * 4 + tap
                    op("dve", lambda e, blk=blk, tap=tap, col=col: e.tensor_scalar(out=dgv[:, blk, tap, :], in0=id32, scalar1=featc[:, col:col + 1],
                                                                                   scalar2=None, op0=ALU.mult), reads=["id32", "featc"], writes=["dg"])
            for h in range(8):
                op("dve", lambda e, h=h: e.tensor_scalar(out=dDv[:, h, :], in0=id32, scalar1=Db[:, g * 8 + h:g * 8 + h + 1], scalar2=None, op0=ALU.mult),
                   reads=["id32", "Db"], writes=["dD"])
            op("pool", lambda e: e.memset(stgv[:, :, 0:1], 0.0), writes=["stg"])
            op("pool", lambda e: e.memset(stgv[:, :, 2049:2050], 0.0), writes=["stg"])
            dh = dsem()
            for d in range(2):
                dma("sp", dh, lambda e, d=d, g=g: e.dma_start(out=hst.rearrange("p (b n) -> p b n", b=4),
                                                             in_=h0_d[d, g * 512:(g + 1) * 512, :].rearrange("(b p) n -> p b n", p=128)), writes=["hst"])
                for b in range(4):
                    op("pe", lambda e, b=b: e.transpose(ps[7][:, b * 128:(b + 1) * 128], hst[:, b * 128:(b + 1) * 128], id32), reads=["hst", "id32"], writes=[PS[7]])
                op("dve", lambda e, d=d: e.tensor_copy(H[d], ps[7]), reads=[PS[7]], writes=[f"H{d}"])
            for t in range(4):
                ts = slice(t * 512, (t + 1) * 512)
                for blk in range(6):
                    pp = (t * 6 + blk) % 4
                    for k in range(8):
                        op("pe", lambda e, blk=blk, k=k, pp=pp, ts=ts: e.matmul(ps[pp], lhsT=wBxv[:, k, blk * 128:(blk + 1) * 128], rhs=hTv[:, k, ts],
                                                                              start=(k == 0), stop=(k == 7)), reads=HT_ALL[4 * t:4 * t + 4] + ["wBx"], writes=[PS[pp]], inc=(k == 7))
                    if blk % 2 == 0:
                        op("act", lambda e, blk=blk, pp=pp, t=t: e.copy(stgv[:, blk, 1 + t * 512:1 + (t + 1) * 512], ps[pp]), reads=[PS[pp]], writes=["stg"])
                    else:
                        op("dve", lambda e, blk=blk, pp=pp, t=t: e.tensor_copy(stgv[:, blk, 1 + t * 512:1 + (t + 1) * 512], ps[pp]), reads=[PS[pp]], writes=["stg"])
            op("dve", lambda e: e.tensor_scalar(out=xfLv, in0=stgv[:, :, 0:2048:256], scalar1=nfs, scalar2=None, op0=ALU.mult), reads=["stg", "pcs"], writes=["xfL"])
            op("dve", lambda e: e.tensor_scalar(out=xfRv, in0=stgv[:, :, 257:2050:256], scalar1=nfs, scalar2=None, op0=ALU.mult), reads=["stg", "pcs"], writes=["xfR"])
            for t in range(4):
                xs_ = t % 2
                xcv = xc[xs_].rearrange("p (b t) -> p b t", b=4)
                for blk in range(6):
                    pp = 4 + (t * 6 + blk) % 2
                    for tap in range(3):
                        op("pe", lambda e, blk=blk, tap=tap, pp=pp, t=t: e.matmul(ps[pp], lhsT=dgv[:, blk, tap, :], rhs=stgv[:, blk, t * 512 + tap:t * 512 + tap + 512],
                                                                                  start=(tap == 0), stop=False), reads=["dg", "stg"], writes=[PS[pp]], inc=False)
                    op("pe", lambda e, blk=blk, pp=pp, t=t: e.matmul(ps[pp][:, 0:512:256], lhsT=dgv[:, blk, 0, :], rhs=xfLv[:, blk, 2 * t:2 * t + 2], start=False, stop=False),
                       reads=["dg", "xfL"], writes=[PS[pp]], inc=False)
                    op("pe", lambda e, blk=blk, pp=pp, t=t: e.matmul(ps[pp][:, 255:512:256], lhsT=dgv[:, blk, 2, :], rhs=xfRv[:, blk, 2 * t:2 * t + 2], start=False, stop=True),
                       reads=["dg", "xfR"], writes=[PS[pp]], inc=True)
                    bcol = fb + blk * 4 + 3
                    if blk < 4:
                        dst, dk = xcv[:, blk, :], f"xc{xs_}"
                    elif blk == 4:
                        dst, dk = BT[:, t * 512:(t + 1) * 512], f"BT{t}"
                    else:
                        dst, dk = CT[:, t * 512:(t + 1) * 512], f"CT{t}"
                    op("act", lambda e, pp=pp, dst=dst, bcol=bcol: e.activation(out=dst, in_=ps[pp], func=AF.Silu, bias=featc[:, bcol:bcol + 1]),
                       reads=[PS[pp], "featc"], writes=[dk])
                for cc in range(4):
                    c = t * 4 + cc
                    ct = slice(c * 128, (c + 1) * 128)
                    for b in range(4):
                        op("pe", lambda e, b=b, cc=cc, xcv=xcv: e.transpose(psb[6][:, b * 128:(b + 1) * 128], xcv[:, b, cc * 128:(cc + 1) * 128], id16),
                           reads=[f"xc{xs_}", "id16"], writes=[PS[6]])
                    op("dve", lambda e, c=c: e.tensor_copy(xtmv[:, c, :], psb[6][:, 0:512]), reads=[PS[6]], writes=[f"xtm{c}"])
                    op("pe", lambda e, ct=ct: e.transpose(psb[7][:, 0:128], BT[:, ct], id16), reads=[f"BT{t}", "id16"], writes=[PS[7]])
                    op("act", lambda e, c=c: e.copy(Btmv[:, c, :], psb[7][:, 0:128]), reads=[PS[7]], writes=[f"Btm{c}"])
                    op("pe", lambda e, ct=ct: e.matmul(ps[7][:, 128:256], lhsT=BT[:, ct], rhs=CT[:, ct], start=True, stop=True), reads=[f"BT{t}", f"CT{t}"], writes=[PS[7]], inc=True)
                    op("dve", lambda e, c=c: e.tensor_tensor(out=CBmAv[:, c, 0, :], in0=ps[7][:, 128:256], in1=trif, op=ALU.mult), reads=[PS[7], "trif"], writes=[f"CBm{c}"])
                    op("dve", lambda e, c=c: e.tensor_tensor(out=CBmAv[:, c, 1, :], in0=ps[7][:, 128:256], in1=trib, op=ALU.mult), reads=[PS[7], "trib"], writes=[f"CBm{c}"])
                    pz = c % 4
                    for k in range(8):
                        op("pe", lambda e, k=k, ct=ct, pz=pz: e.matmul(ps[pz], lhsT=hTv[:, k, ct], rhs=wBzv[:, k, :], start=(k == 0), stop=(k == 7)),
                           reads=[hk(c), "wBz"], writes=[PS[pz]], inc=(k == 7))
                    op("act", lambda e, c=c, pz=pz: e.activation(out=szAv[:, c, :], in_=ps[pz], func=AF.Silu), reads=[PS[pz]], writes=[f"sz{c}"])

            S.barrier()
            A.off = G_MARK
            NS = 3
            sets = []
            for si in range(NS):
                d_ = {}
                for nm in ("Et", "MT", "Eo", "CE"):
                    d_[nm] = A.alloc(1024, BF16)
                    d_[nm + "v"] = d_[nm].rearrange("p (h i) -> p h i", h=8)
                d_["xw"] = A.alloc(512, BF16)
                d_["tD"] = A.alloc(512, F32)
                d_["cd"] = A.alloc(8, F32)
                d_["Hin"] = A.alloc(512, BF16)
                sets.append(d_)
            so = A.alloc(512, F32)
            ytb = [A.alloc(512, F32) for _ in range(2)]
            ygo = [A.alloc(512, BF16) for _ in range(2)]
            sqs = A.alloc(512, F32)
            dso = dsem()
            dyg = [dsem(), dsem()]
            units = []
            for k in range(NCH):
                units += [(k, 0), (NCH - 1 - k, 1)]
            visited = set()
            nfin = [0]
            nst = [0]
            pend = []
            firstv = {}
            Hs = [A.alloc(512, F32) for _ in range(2)]

            def frontPE(ui):
                c, d = units[ui]
                pa = ui % 2
                ct = slice(c * 128, (c + 1) * 128)
                r0 = d * 32 + g * 8
                pA = (2 * pa, 2 * pa + 1)
                for h in range(8):
                    bank = pA[h // 4]
                    o = ps[bank][:, (h % 4) * 128:(h % 4 + 1) * 128]
                    sel = e64[0:64, r0 + h:r0 + h + 1].broadcast_to([64, 128])
                    op("pe", lambda e, o=o, sel=sel: e.matmul(o, lhsT=sel, rhs=hi16[0:64, ct], start=True, stop=False), reads=["e64", "hi16"], writes=[PS[bank]], inc=False)
                    op("pe", lambda e, o=o, sel=sel: e.matmul(o, lhsT=sel, rhs=lo16[0:64, ct], start=False, stop=True), reads=["e64", "lo16"], writes=[PS[bank]], inc=True)

            def front(ui):
                c, d = units[ui]
                si, pa = ui % NS, ui % 2
                B = sets[si]
                ct = slice(c * 128, (c + 1) * 128)
                r0 = d * 32 + g * 8
                lastc = 127 if d == 0 else 0
                keep = (keepf if d == 0 else keepb)[:, c:c + 1]
                pA = (2 * pa, 2 * pa + 1)
                for hb in range(2):
                    bank = pA[hb]
                    op("act", lambda e, hb=hb, bank=bank: e.activation(out=B["Eo"][:, hb * 512:(hb + 1) * 512], in_=ps[bank], func=AF.Exp), reads=[PS[bank]], writes=[f"Eo{si}"])
                    op("act", lambda e, hb=hb, bank=bank: e.activation(out=B["cd"][:, hb * 4:(hb + 1) * 4], in_=ps[bank][:, lastc:512:128], func=AF.Exp,
                                                                       bias=kbias[:, d * 16 + c:d * 16 + c + 1]),
                       reads=[PS[bank], "kbias"], writes=[f"cd{si}"])
                op("dve", lambda e: e.tensor_tensor(out=B["tD"].rearrange("p (h i) -> p h i", h=4), in0=ps[pA[1]].rearrange("p (h i) -> p h i", h=4),
                                                    in1=btmv[:, c, r0 + 4:r0 + 8].unsqueeze(2).broadcast_to([128, 4, 128]), op=ALU.add),
                   reads=[PS[pA[1]], "btm"], writes=[f"tD{si}"])
                for h in range(4):
                    bank = pA[0]
                    o = ps[bank][:, h * 128:(h + 1) * 128]
                    op("act", lambda e, h=h, o=o: e.activation(out=B["Etv"][:, h, :], in_=o, func=AF.Exp, bias=btmv[:, c, r0 + h:r0 + h + 1]),
                       reads=[PS[bank], "btm"], writes=[f"Et{si}"])
                op("act", lambda e: e.activation(out=B["Et"][:, 512:1024], in_=B["tD"], func=AF.Exp), reads=[f"tD{si}"], writes=[f"Et{si}"])
                op("dve", lambda e: e.tensor_tensor(out=B["CEv"], in0=B["Eov"], in1=CT[:, ct].unsqueeze(1).broadcast_to([128, 8, 128]), op=ALU.mult),
                   reads=[f"Eo{si}", f"CT{c // 4}"], writes=[f"CE{si}"])
                op("dve", lambda e: e.tensor_scalar(out=B["Et"], in0=B["Et"], scalar1=1e30, scalar2=None, op0=ALU.min), reads=[f"Et{si}"], writes=[f"Et{si}"])
                op("dve", lambda e: e.tensor_tensor(out=B["MTv"], in0=B["Etv"], in1=CBmAv[:, c, d, :].unsqueeze(1).broadcast_to([128, 8, 128]), op=ALU.mult),
                   reads=[f"Et{si}", f"CBm{c}"], writes=[f"MT{si}"])
                op("dve", lambda e: e.tensor_tensor(out=B["xw"].rearrange("p (h q) -> p h q", h=8), in0=xtmv[:, c, :].rearrange("p (h q) -> p h q", h=8),
                                                    in1=B["Etv"][:, :, lastc:lastc + 1].broadcast_to([128, 8, 64]), op=ALU.mult), reads=[f"xtm{c}", f"Et{si}"], writes=[f"xw{si}"])

            def back(ui):
                c, d = units[ui]
                si, pa = ui % NS, ui % 2
                B = sets[si]
                pY, pS = 4 + pa, 6 + pa
                firstv[ui] = c not in visited
                visited.add(c)
                keep = (keepf if d == 0 else keepb)[:, c:c + 1]
                op("dve", lambda e: e.tensor_scalar(out=B["Hin"], in0=H[d], scalar1=keep, scalar2=None, op0=ALU.mult), reads=[f"H{d}", "pcs"], writes=[f"Hin{si}"])
                for h in range(8):
                    o = ps[pY][:, h * 64:(h + 1) * 64]
                    op("pe", lambda e, h=h, o=o: e.matmul(o, lhsT=B["MTv"][:, h, :], rhs=xtmv[:, c, h * 64:(h + 1) * 64], start=True, stop=False),
                       reads=[f"MT{si}", f"xtm{c}"], writes=[PS[pY]], inc=False)
                    second = not firstv[ui]
                    op("pe", lambda e, h=h, o=o: e.matmul(o, lhsT=B["CEv"][:, h, :], rhs=B["Hin"][:, h * 64:(h + 1) * 64], start=False, stop=(d == 1 and not second)),
                       reads=[f"CE{si}", f"Hin{si}"], writes=[PS[pY]], inc=(d == 1 and not second))
                    if d == 0:
                        op("pe", lambda e, h=h, o=o: e.matmul(o, lhsT=dDv[:, h, :], rhs=xtmv[:, c, h * 64:(h + 1) * 64], start=False, stop=(not second)),
                           reads=["dD", f"xtm{c}"], writes=[PS[pY]], inc=(not second))
                    if second:
                        op("pe", lambda e, h=h, o=o: e.matmul(o, lhsT=id16, rhs=yaccv[:, c, h * 64:(h + 1) * 64], start=False, stop=True),
                           reads=["id16", f"yacc{c}"], writes=[PS[pY]], inc=True)
                op("pe", lambda e: e.matmul(ps[pS], lhsT=Btmv[:, c, :], rhs=B["xw"], start=True, stop=True), reads=[f"Btm{c}", f"xw{si}"], writes=[PS[pS]], inc=True)
                op("dve", lambda e: e.tensor_tensor(out=H[d].rearrange("p (h q) -> p h q", h=8), in0=H[d].rearrange("p (h q) -> p h q", h=8),
                                                    in1=B["cd"].unsqueeze(2).broadcast_to([128, 8, 64]), op=ALU.mult), reads=[f"H{d}", f"cd{si}"], writes=[f"H{d}"])
                op("dve", lambda e: e.tensor_tensor(out=H[d], in0=H[d], in1=ps[pS], op=ALU.add), reads=[f"H{d}", PS[pS]], writes=[f"H{d}"])

            def backB(ui):
                c, d = units[ui]
                si, pa = ui % NS, ui % 2
                B = sets[si]
                pY, pS = 4 + pa, 6 + pa
                first = firstv[ui]
                while pend:
                    hs_i, d2, sidx2 = pend.pop(0)
                    for b in range(4):
                        op("pe", lambda e, b=b: e.transpose(ps[pS][:, b * 128:(b + 1) * 128], Hs[hs_i][:, b * 128:(b + 1) * 128], id32), reads=[f"Hs{hs_i}", "id32"], writes=[PS[pS]])
                    op("act", lambda e: e.copy(so, ps[pS]), reads=[PS[pS]], writes=["so"])
                    dma("sp", dso, lambda e: e.dma_start(out=st_d[d2][sidx2, g * 512:(g + 1) * 512, :].rearrange("(b p) n -> p b n", p=128),
                                                        in_=so.rearrange("p (b n) -> p b n", b=4)), reads=["so"], writes=[f"st{d2}_{sidx2}_{g}"])
                if (d == 0 and c % 2 == 1) or (d == 1 and c % 2 == 0):
                    hs_i = nst[0] % 2
                    nst[0] += 1
                    op("act", lambda e: e.copy(Hs[hs_i], H[d]), reads=[f"H{d}"], writes=[f"Hs{hs_i}"])
                    pend.append((hs_i, d, c // 2))
                if first:
                    op("act", lambda e: e.copy(yaccv[:, c, :], ps[pY]), reads=[PS[pY]], writes=[f"yacc{c}"])
                else:
                    s = nfin[0] % 2
                    nfin[0] += 1
                    yt = ytb[s]
                    op("dve", lambda e: e.tensor_tensor(out=ygo[s], in0=ps[pY], in1=szAv[:, c, :], op=ALU.mult), reads=[PS[pY], f"sz{c}"], writes=[f"ygo{s}"])
                    op("act", lambda e: e.activation(out=sqs, in_=ygo[s], func=AF.Square, accum_out=ssy[:, c * 4 + g:c * 4 + g + 1]), reads=[f"ygo{s}"], writes=["sqs", f"ssy{c}"])
                    dma("sp", dyg[s], lambda e: e.dma_start(out=yg_d[c * 128:(c + 1) * 128, g * 512:(g + 1) * 512], in_=ygo[s]),
                        reads=[f"ygo{s}"], writes=[f"yg_d{c}"])

            frontPE(0)
            for t_ in range(len(units) + 2):
                if t_ + 1 < len(units):
                    frontPE(t_ + 1)
                if t_ >= 2:
                    back(t_ - 2)
                if t_ < len(units):
                    front(t_)
                if t_ >= 2:
                    backB(t_ - 2)
            while pend:
                hs_i, d2, sidx2 = pend.pop(0)
                for b in range(4):
                    op("pe", lambda e, b=b: e.transpose(ps[6][:, b * 128:(b + 1) * 128], Hs[hs_i][:, b * 128:(b + 1) * 128], id32), reads=[f"Hs{hs_i}", "id32"], writes=[PS[6]])
                op("act", lambda e: e.copy(so, ps[6]), reads=[PS[6]], writes=["so"])
                dma("sp", dso, lambda e: e.dma_start(out=st_d[d2][sidx2, g * 512:(g + 1) * 512, :].rearrange("(b p) n -> p b n", p=128),
                                                    in_=so.rearrange("p (b n) -> p b n", b=4)), reads=["so"], writes=[f"st{d2}_{sidx2}_{g}"])

        if stop_after == "B":
            S.barrier()
            dtap("ssy", ssy, "ssy", F32)
            S.final_wait("sp")
            S.emit(nc, st)
            return nc
        S.barrier()
        A.off = P_MARK
        rmode[0] = "pool"
        wbs = A.alloc(16 * 1024, BF16)
        wbsv = wbs.rearrange("p (k n) -> p k n", k=16)
        wga = A.alloc(8 * 1024, BF16)
        wgav = wga.rearrange("p (k n) -> p k n", k=8)
        wout = A.alloc(8 * 1024, BF16)
        woutv = wout.rearrange("p (k n) -> p k n", k=8)
        ygt = [A.alloc(2048, BF16) for _ in range(3)]
        msl = [A.alloc(1024, BF16) for _ in range(3)]
        xl = [A.alloc(1024, F32) for _ in range(3)]
        yTc = [A.alloc(16 * 128, BF16) for _ in range(2)]
        sga = [A.alloc(1024, F32) for _ in range(2)]
        t1 = [A.alloc(1024, F32) for _ in range(2)]
        mg = [A.alloc(1024, BF16) for _ in range(2)]
        mgT = [A.alloc(1024, BF16) for _ in range(2)]
        x1 = [A.alloc(1024, F32) for _ in range(3)]
        scr = [A.alloc(1024, F32) for _ in range(3)]
        rsy = A.alloc(NCH, F32)
        tnc = A.alloc(NCH * 4, F32)
        dwc = dsem()
        dwc2, dwc3 = dsem(), dsem()
        dma("pool", dwc, lambda e: e.dma_start(out=wgav, in_=wga_d.rearrange("(k p) n -> p k n", p=128)), writes=["wga"])
        dma("pool", dwc2, lambda e: e.dma_start(out=wbsv[:, 0:8, :], in_=wbs_d[0:1024, :].rearrange("(k p) n -> p k n", p=128)), writes=["wbs"])
        dma("pool", dwc2, lambda e: e.dma_start(out=wbsv[:, 8:16, :], in_=wbs_d[1024:2048, :].rearrange("(k p) n -> p k n", p=128)), writes=["wbs"])
        dma("pool", dwc3, lambda e: e.dma_start(out=woutv, in_=wout_d.rearrange("(k p) n -> p k n", p=128)), writes=["wout"])
        for kb in range(16):
            op("dve", lambda e, kb=kb: e.tensor_scalar(out=wbsv[:, kb, :], in0=wbsv[:, kb, :], scalar1=featc[:, R_SSDN + kb:R_SSDN + kb + 1], scalar2=None, op0=ALU.mult),
               reads=["wbs", "featc"], writes=["wbs"])
        op("dve", lambda e: e.tensor_reduce(out=rsy, in_=ssy.rearrange("p (c g) -> p c g", g=4), axis=mybir.AxisListType.X, op=ALU.add), reads=["ssy"], writes=["rsy"])
        rstd_from_ss(rsy, 2048, rsy, "rsy")
        dl = [dsem(), dsem(), dsem()]
        dx1 = [dsem(), dsem(), dsem()]

        def c_load(c):
            s3 = c % 3
            dma("sp", dl[s3], lambda e: e.dma_start(out=ygt[s3], in_=yg_d[c * 128:(c + 1) * 128, :]), reads=[f"yg_d{c}"], writes=[f"ygt{s3}"])
            dma("sp", dl[s3], lambda e: e.dma_start(out=msl[s3], in_=msgu_d[c * 128:(c + 1) * 128, :]), reads=[f"msgu_d{c}"], writes=[f"msl{s3}"])
            dma("sp", dl[s3], lambda e: e.dma_start(out=xl[s3], in_=xin[c * 128:(c + 1) * 128, :]), writes=[f"xl{s3}"])

        def c_front(c):
            s = c % 2
            s3 = c % 3
            ct = slice(c * 128, (c + 1) * 128)
            yTcv = yTc[s].rearrange("p (k t) -> p k t", k=16)
            for nb in range(2):
                for k in range(8):
                    op("pe", lambda e, nb=nb, k=k: e.matmul(ps[2 + nb], lhsT=hTv[:, k, ct], rhs=wgav[:, k, nb * 512:(nb + 1) * 512],
                                                            start=(k == 0), stop=(k == 7)), reads=[hk(c), "wga"], writes=[PS[2 + nb]], inc=(k == 7))
                sl = slice(nb * 512, (nb + 1) * 512)
                op("act", lambda e, nb=nb, sl=sl: e.activation(out=sga[s][:, sl], in_=ps[2 + nb], func=AF.Sigmoid), reads=[PS[2 + nb]], writes=[f"sga{s}"])
            for kb in range(16):
                pp = kb // 8
                op("pe", lambda e, kb=kb, pp=pp: e.transpose(psb[pp][:, (kb % 8) * 128:(kb % 8 + 1) * 128], ygt[s3][:, kb * 128:(kb + 1) * 128], id16),
                   reads=[f"ygt{s3}", "id16"], writes=[PS[pp]])
            op("act", lambda e: e.copy(yTc[s][:, 0:1024], psb[0]), reads=[PS[0]], writes=[f"yTc{s}a"])
            op("dve", lambda e: e.tensor_copy(yTc[s][:, 1024:2048], psb[1]), reads=[PS[1]], writes=[f"yTc{s}b"])
            for nb in range(2):
                for kb in range(16):
                    op("pe", lambda e, nb=nb, kb=kb: e.matmul(ps[nb], lhsT=yTcv[:, kb, :], rhs=wbsv[:, kb, nb * 512:(nb + 1) * 512],
                                                              start=(kb == 0), stop=(kb == 15)), reads=[f"yTc{s}" + ("a" if kb < 8 else "b"), "wbs"], writes=[PS[nb]], inc=(kb == 15))
                sl = slice(nb * 512, (nb + 1) * 512)
                op("dve", lambda e, nb=nb, sl=sl: e.scalar_tensor_tensor(out=t1[s][:, sl], in0=ps[nb], scalar=rsy[:, c:c + 1], in1=sga[s][:, sl],
                                                                          op0=ALU.mult, op1=ALU.mult), reads=[PS[nb], "rsy", f"sga{s}"], writes=[f"t1{s}"])
            op("dve", lambda e: e.tensor_tensor(out=mg[s], in0=t1[s], in1=msl[s3], op=ALU.add), reads=[f"t1{s}", f"msl{s3}"], writes=[f"mg{s}"])

        def c_tail1(c):
            s = c % 2
            s3 = c % 3
            q = c % 3
            mgTv = mgT[s].rearrange("p (k t) -> p k t", k=8)
            tk = f"tnc{c}"
            for k in range(8):
                op("pe", lambda e, k=k: e.transpose(psb[4][:, k * 128:(k + 1) * 128], mg[s][:, k * 128:(k + 1) * 128], id16), reads=[f"mg{s}", "id16"], writes=[PS[4]])
            op("act", lambda e: e.copy(mgT[s], psb[4]), reads=[PS[4]], writes=[f"mgT{s}"])

        def c_tail1b(c):
            s = c % 2
            s3 = c % 3
            q = c % 3
            mgTv = mgT[s].rearrange("p (k t) -> p k t", k=8)
            tk = f"tnc{c}"
            for nb in range(2):
                for k in range(8):
                    op("pe", lambda e, nb=nb, k=k: e.matmul(ps[5 + nb], lhsT=mgTv[:, k, :], rhs=woutv[:, k, nb * 512:(nb + 1) * 512],
                                                            start=(k == 0), stop=(k == 7)), reads=[f"mgT{s}", "wout"], writes=[PS[5 + nb]], inc=(k == 7))
                op("act", lambda e, nb=nb: e.activation(out=scr[q][:, nb * 512:(nb + 1) * 512], in_=ps[5 + nb], func=AF.Square, accum_out=tnc[:, c * 4 + nb:c * 4 + nb + 1]),
                   reads=[PS[5 + nb]], writes=[f"scr{q}", tk])
            rc = tnc[:, c * 4 + 2:c * 4 + 3]
            op("dve", lambda e: e.tensor_tensor(out=rc, in0=tnc[:, c * 4:c * 4 + 1], in1=tnc[:, c * 4 + 1:c * 4 + 2], op=ALU.add), reads=[tk], writes=[tk])
            rstd_from_ss(rc, 1024, rc, tk)
            for nb in range(2):
                sl = slice(nb * 512, (nb + 1) * 512)
                op("dve", lambda e, nb=nb, sl=sl: e.scalar_tensor_tensor(out=t1[s][:, sl], in0=ps[5 + nb], scalar=rc, in1=gw1[:, sl],
                                                                          op0=ALU.mult, op1=ALU.mult), reads=[PS[5 + nb], tk, "g1"], writes=[f"t1{s}"])
            op("dve", lambda e: e.tensor_tensor(out=x1[q], in0=t1[s], in1=xl[s3], op=ALU.add), reads=[f"t1{s}", f"xl{s3}"], writes=[f"x1{q}"])
            dma("pool", dx1[q], lambda e: e.dma_start(out=yout[c * 128:(c + 1) * 128, :], in_=x1[q]), reads=[f"x1{q}"], writes=[f"yout{c}"])

        def c_tail2a(c):
            s = c % 3
            tk = f"tnc{c}"
            r2 = tnc[:, c * 4 + 3:c * 4 + 4]
            op("act", lambda e: e.activation(out=scr[s], in_=x1[s], func=AF.Square, accum_out=r2), reads=[f"x1{s}"], writes=[f"scr{s}", tk])
            rstd_from_ss(r2, 1024, r2, tk)
            op("dve", lambda e: e.tensor_scalar(out=scr[s], in0=x1[s], scalar1=r2, scalar2=None, op0=ALU.mult), reads=[f"x1{s}", tk], writes=[f"scr{s}"])

        def c_tail2b(c):
            s = c % 3
            for half in range(2):
                pb_ = 7 if half == 0 else 6
                for k4 in range(4):
                    k = half * 4 + k4
                    op("pe", lambda e, k=k, k4=k4, pb_=pb_: e.transpose(ps[pb_][:, k4 * 128:(k4 + 1) * 128], scr[s][:, k * 128:(k + 1) * 128], id32),
                       reads=[f"scr{s}", "id32"], writes=[PS[pb_]])
                for k4 in range(4):
                    k = half * 4 + k4
                    if k4 % 2 == 0:
                        op("act", lambda e, k=k, k4=k4, pb_=pb_: e.activation(out=hTv[:, k, c * 128:(c + 1) * 128], in_=ps[pb_][:, k4 * 128:(k4 + 1) * 128],
                                                                            func=AF.Identity, scale=g2[:, k:k + 1], bias=modT[:, 16 + k:16 + k + 1]),
                           reads=[PS[pb_], "g12", "modT"], writes=[hk(c)])
                    else:
                        op("dve", lambda e, k=k, k4=k4, pb_=pb_: e.tensor_scalar(out=hTv[:, k, c * 128:(c + 1) * 128], in0=ps[pb_][:, k4 * 128:(k4 + 1) * 128],
                                                                                scalar1=g2[:, k:k + 1], scalar2=modT[:, 16 + k:16 + k + 1], op0=ALU.mult, op1=ALU.add),
                           reads=[PS[pb_], "g12", "modT"], writes=[hk(c)])

        c_load(0)
        c_load(1)
        c_front(0)
        for c in range(NCH + 2):
            if c + 2 < NCH:
                c_load(c + 2)
            if c >= 2:
                c_tail2a(c - 2)
            if c + 1 < NCH:
                c_front(c + 1)
            if c < NCH:
                c_tail1(c)
            if c >= 2:
                c_tail2b(c - 2)
            if c < NCH:
                c_tail1b(c)

        if stop_after == "C":
            dtap("hT", hT, HT_ALL, BF16)
            S.final_wait("sp")
            S.emit(nc, st)
            return nc
        S.barrier()
        A.off = P_MARK
        actT = A.alloc(NFB * NT, BF16)
        actTv = actT.rearrange("p (i t) -> p i t", i=NFB)
        F_MARK = A.off
        SW = 65 + NT + 65
        wu = [A.alloc(8 * 128, BF16) for _ in range(2)]
        fstg = [A.alloc(SW, BF16) for _ in range(2)]
        dgf = [A.alloc(9 * 128, BF16) for _ in range(2)]
        xfl = [A.alloc(3 * 32, BF16) for _ in range(2)]
        xfr = [A.alloc(3 * 32, BF16) for _ in range(2)]
        gbuf = A.alloc(NT, BF16)
        dwu = [dsem(), dsem()]
        for s in range(2):
            op("pool", lambda e, s=s: e.memset(fstg[s], 0.0), writes=[f"fstg{s}"])
        A.top = A.nbytes
        wd = A.alloc_top(NFB * 1024, BF16)
        wdv = wd.rearrange("p (i n) -> p i n", i=NFB)
        assert A.top >= A.off, (A.top, A.off)
        dwd2 = dsem()
        def p2_hdr(j):
            i, part = j // 2, j % 2
            return i, part, part * NFB + i, j % 2

        def p2_proj(j):
            i, part, blk, s = p2_hdr(j)
            if j == 2 * (NFB - 4):
                for i0 in range(0, NFB, 11):
                    dma("pool", dwd2, lambda e, i0=i0: e.dma_start(out=wdv[:, i0:i0 + 11, :], in_=wdn_d[i0 * 128:(i0 + 11) * 128, :].rearrange("(i p) n -> p i n", p=128)), writes=["wd"])
            wuv = wu[s].rearrange("p (k n) -> p k n", k=8)
            dgv2 = dgf[s].rearrange("p (t i) -> p t i", t=9)
            dma("pool", dwu[s], lambda e, blk=blk, wuv=wuv: e.dma_start(out=wuv, in_=wup_d[:, blk * 128:(blk + 1) * 128].rearrange("(k p) n -> p k n", p=128)),
                writes=[f"wu{s}"])
            fcb = R_FCONV + blk * 10
            op("dve", lambda e, dgv2=dgv2, fcb=fcb: e.tensor_tensor(out=dgv2, in0=id32.unsqueeze(1).broadcast_to([128, 9, 128]),
                                                                    in1=featc[:, fcb:fcb + 9].unsqueeze(2).broadcast_to([128, 9, 128]), op=ALU.mult),
               reads=["id32", "featc"], writes=[f"dgf{s}"])
            for t in range(4):
                ts = slice(t * 512, (t + 1) * 512)
                for k in range(8):
                    op("pe", lambda e, t=t, k=k, wuv=wuv, ts=ts: e.matmul(ps[t], lhsT=wuv[:, k, :], rhs=hTv[:, k, ts], start=(k == 0), stop=(k == 7)),
                       reads=HT_ALL[4 * t:4 * t + 4] + [f"wu{s}"], writes=[PS[t]], inc=(k == 7))
                if t % 2 == 0:
                    op("act", lambda e, t=t, s=s: e.copy(fstg[s][:, 65 + t * 512:65 + (t + 1) * 512], ps[t]), reads=[PS[t]], writes=[f"fstg{s}"])
                else:
                    op("dve", lambda e, t=t, s=s: e.tensor_copy(fstg[s][:, 65 + t * 512:65 + (t + 1) * 512], ps[t]), reads=[PS[t]], writes=[f"fstg{s}"])

        def p2_conv(j):
            i, part, blk, s = p2_hdr(j)
            dgv2 = dgf[s].rearrange("p (t i) -> p t i", t=9)
            fcb = R_FCONV + blk * 10
            xflv = xfl[s].rearrange("p (r m) -> p r m", r=3)
            xfrv = xfr[s].rearrange("p (r m) -> p r m", r=3)
            for dr in (-1, 0, 1):
                op("pool", lambda e, dr=dr, s=s, xflv=xflv: e.tensor_tensor(out=xflv[:, dr + 1, :], in0=fstg[s][:, 64 * dr + 64:64 * dr + 64 + 1985:64],
                                                                             in1=nff[:, 0:32], op=ALU.mult), reads=[f"fstg{s}", "pcs"], writes=[f"xfl{s}"])
                op("pool", lambda e, dr=dr, s=s, xfrv=xfrv: e.tensor_tensor(out=xfrv[:, dr + 1, :], in0=fstg[s][:, 64 * dr + 129:64 * dr + 129 + 1985:64],
                                                                             in1=nff[:, 1:33], op=ALU.mult), reads=[f"fstg{s}", "pcs"], writes=[f"xfr{s}"])
            for t in range(4):
                pp = 4 + t
                n = 0
                for dr in (-1, 0, 1):
                    for dcc in (-1, 0, 1):
                        tap = (dr + 1) * 3 + dcc + 1
                        o0 = 65 + t * 512 + 64 * dr + dcc
                        op("pe", lambda e, tap=tap, o0=o0, pp=pp, s=s, dgv2=dgv2, n=n: e.matmul(ps[pp], lhsT=dgv2[:, tap, :], rhs=fstg[s][:, o0:o0 + 512],
                                                                                                 start=(n == 0), stop=False), reads=[f"dgf{s}", f"fstg{s}"], writes=[PS[pp]], inc=False)
                        n += 1
                for dr in (-1, 0, 1):
                    op("pe", lambda e, dr=dr, pp=pp, t=t, dgv2=dgv2, xflv=xflv: e.matmul(ps[pp][:, 0:512:64], lhsT=dgv2[:, (dr + 1) * 3, :], rhs=xflv[:, dr + 1, 8 * t:8 * t + 8],
                                                                                          start=False, stop=False), reads=[f"dgf{s}", f"xfl{s}"], writes=[PS[pp]], inc=False)
                    op("pe", lambda e, dr=dr, pp=pp, t=t, dgv2=dgv2, xfrv=xfrv: e.matmul(ps[pp][:, 63:512:64], lhsT=dgv2[:, (dr + 1) * 3 + 2, :], rhs=xfrv[:, dr + 1, 8 * t:8 * t + 8],
                                                                                          start=False, stop=(dr == 1)), reads=[f"dgf{s}", f"xfr{s}"], writes=[PS[pp]], inc=(dr == 1))
                ts = slice(t * 512, (t + 1) * 512)
                if part == 0:
                    op("act", lambda e, pp=pp, ts=ts, fcb=fcb: e.activation(out=gbuf[:, ts], in_=ps[pp], func=AF.Gelu_apprx_tanh, bias=featc[:, fcb + 9:fcb + 10]),
                       reads=[PS[pp], "featc"], writes=["gbuf"])
                else:
                    op("dve", lambda e, pp=pp, ts=ts, fcb=fcb, i=i: e.scalar_tensor_tensor(out=actTv[:, i, ts], in0=ps[pp], scalar=featc[:, fcb + 9:fcb + 10], in1=gbuf[:, ts],
                                                                                          op0=ALU.add, op1=ALU.mult), reads=[PS[pp], "featc", "gbuf"], writes=["actT"])


        p2_proj(0)
        for j in range(2 * NFB):
            if j + 1 < 2 * NFB:
                p2_proj(j + 1)
            p2_conv(j)

        if stop_after == "2":
            dtap("actT", actT, "actT", BF16)
            S.final_wait("sp")
            S.emit(nc, st)
            return nc
        S.barrier(exclude=(dwd2,))
        A.off = F_MARK
        x1l = [A.alloc(1024, F32) for _ in range(2)]
        yo = [A.alloc(1024, F32) for _ in range(2)]
        dl2 = [dsem(), dsem()]
        dy = [dsem(), dsem()]
        for c in range(NCH):
            s = c % 2
            ct = slice(c * 128, (c + 1) * 128)
            dma("sp", dl2[s], lambda e, s=s, c=c: e.dma_start(out=x1l[s], in_=yout[c * 128:(c + 1) * 128, :]), reads=[f"yout{c}"], writes=[f"x1l{s}"])
            pb = 2 * s
            for nb in range(2):
                for i in range(NFB):
                    op("pe", lambda e, nb=nb, i=i, pb=pb: e.matmul(ps[pb + nb], lhsT=actTv[:, i, ct], rhs=wdv[:, i, nb * 512:(nb + 1) * 512],
                                                                  start=(i == 0), stop=(i == NFB - 1)), reads=["actT", "wd"], writes=[PS[pb + nb]], inc=(i == NFB - 1))
                op("act", lambda e, nb=nb, pb=pb, s=s: e.activation(out=yo[s][:, nb * 512:(nb + 1) * 512], in_=ps[pb + nb], func=AF.Square, accum_out=tiny[:, 8 + nb:9 + nb]),
                   reads=[PS[pb + nb]], writes=[f"yo{s}", "tiny"])
            op("dve", lambda e: e.tensor_tensor(out=tiny[:, 10:11], in0=tiny[:, 8:9], in1=tiny[:, 9:10], op=ALU.add), reads=["tiny"], writes=["tiny"])
            rstd_from_ss(tiny[:, 10:11], 1024, tiny[:, 10:11], "tiny")
            for nb in range(2):
                sl = slice(nb * 512, (nb + 1) * 512)
                op("dve", lambda e, nb=nb, sl=sl, pb=pb, s=s: e.scalar_tensor_tensor(out=yo[s][:, sl], in0=ps[pb + nb], scalar=tiny[:, 10:11], in1=gw2[:, sl],
                                                                                  op0=ALU.mult, op1=ALU.mult), reads=[PS[pb + nb], "tiny", "g2"], writes=[f"yo{s}"])
            op("pool", lambda e, s=s: e.tensor_tensor(out=yo[s], in0=yo[s], in1=x1l[s], op=ALU.add), reads=[f"yo{s}", f"x1l{s}"], writes=[f"yo{s}"])
            dma("pool", dy[s], lambda e, s=s, c=c: e.dma_start(out=yout[c * 128:(c + 1) * 128, :], in_=yo[s]), reads=[f"yo{s}"], writes=[f"yout{c}"])

        S.final_wait("sp")
        S.emit(nc, st)
    return nc


def _prep_inputs(I):
    f = np.float32
    xp, xsm = I["x_prompt"], I["x_sample"]
    w_in = I["w_in"][0]
    Z, XBC = 2048, 2048 + 3072
    DT = XBC + 64
    U, V, GA = DT + 1024, DT + 2048, DT + 3072
    z_c = w_in[:, 0:Z]
    xbc_c = w_in[:, Z:XBC]
    wdt = np.ascontiguousarray(w_in[:, XBC:DT])
    u_c, v_c, ga_c, gb_c = w_in[:, DT:U], w_in[:, U:V], w_in[:, V:GA], w_in[:, GA:GA + 1024]
    wA = np.ascontiguousarray(np.concatenate([u_c, v_c, gb_c], axis=1))
    wB = np.stack([np.concatenate([z_c[:, g * 512:(g + 1) * 512], xbc_c[:, g * 512:(g + 1) * 512],
                                   xbc_c[:, 2048 + g * 128:2048 + (g + 1) * 128], xbc_c[:, 2560 + g * 128:2560 + (g + 1) * 128]], axis=1)
                   for g in range(4)]).astype(f)
    wmod_full = I["w_mod"][0]
    wmod = np.ascontiguousarray(wmod_full.reshape(1024, 6, 1024).transpose(1, 0, 2))
    bmod = I["b_mod"][0].reshape(6, 1024)
    cw, cb = I["ssd_conv_w"][0], I["ssd_conv_b"][0]
    fw, fbias = I["ffn_conv_w"][0].reshape(9, 5632), I["ffn_conv_b"][0]
    fw_prompt = np.zeros_like(fw)
    fw_prompt[3:6] = fw[3:6]
    shared = dict(
        wmod=wmod, wA=wA, wB=wB, wdt=wdt, wga=np.ascontiguousarray(ga_c),
        wsT=np.ascontiguousarray(I["sgu_w"][0].transpose(2, 0, 1)), sgub=np.ascontiguousarray(I["sgu_b"][0].reshape(1, 1024)),
        wbg=I["w_branch_sgu"][0], wbs=I["w_branch_ssd"][0], wout=I["w_out"][0], wup=I["ffn_w_up"][0], wdn=I["ffn_w_down"][0],
    )
    rowb = np.zeros((1, NRB), f)
    rowb[0, B_NMPOST:B_NMPOST + 1024] = I["norm_mix_post"][0]
    rowb[0, B_NFPOST:B_NFPOST + 1024] = I["norm_ffn_post"][0]
    rowb[0, B_G1:B_G1 + 1024] = bmod[2]
    rowb[0, B_G2:B_G2 + 1024] = bmod[5]
    rowb[0, B_SGW:B_SGW + 1024] = I["sgu_norm_w"][0]
    rowb[0, B_SGB:B_SGB + 1024] = I["sgu_norm_b"][0]
    rowb[0, B_SSDN:B_SSDN + 2048] = I["ssd_norm"][0]
    rowb[0, B_SSDD:B_SSDD + 32] = I["ssd_d"][0]

    def rows_for(cvec, fwc):
        r = np.zeros((NROWS, 128), f)
        r[R_C:R_C + 8] = cvec.reshape(8, 128)
        r[R_NMP:R_NMP + 8] = I["norm_mix_pre"][0].reshape(8, 128)
        r[R_NFP:R_NFP + 8] = I["norm_ffn_pre"][0].reshape(8, 128)
        r[R_SH1:R_SH1 + 8] = bmod[0].reshape(8, 128)
        r[R_SC1:R_SC1 + 8] = bmod[1].reshape(8, 128)
        r[R_SH2:R_SH2 + 8] = bmod[3].reshape(8, 128)
        r[R_SC2:R_SC2 + 8] = bmod[4].reshape(8, 128)
        r[R_DTB, 0:64] = I["ssd_dt_bias"][0].reshape(64)
        r[R_ALOG, 0:64] = I["ssd_a_log"][0].reshape(64)
        r[R_SSDN:R_SSDN + 16] = I["ssd_norm"][0].reshape(16, 128)
        for g in range(4):
            chans = [slice(g * 512 + b * 128, g * 512 + (b + 1) * 128) for b in range(4)] + \
                    [slice(2048 + g * 128, 2048 + (g + 1) * 128), slice(2560 + g * 128, 2560 + (g + 1) * 128)]
            for b, sl in enumerate(chans):
                base = R_SCONV + (g * 6 + b) * 4
                r[base:base + 3] = cw[:, sl]
                r[base + 3] = cb[sl]
        for b in range(44):
            base = R_FCONV + b * 10
            r[base:base + 9] = fwc[:, b * 128:(b + 1) * 128]
            r[base + 9] = fbias[b * 128:(b + 1) * 128]
        return r

    maps = []
    for core in range(8):
        pc = np.zeros((128, 66), f)
        if core < 4:
            b = core
            x = xsm[b]
            cvec = I["c"][b]
            h0 = np.stack([I["state_ssd_fwd"][b, 0].reshape(2048, 128), I["state_ssd_bwd"][b, 0].reshape(2048, 128)])
            pc[:, 0:32] = 1.0
            pc[:, 32] = 0.0
            pc[:, 33:66] = -1.0
            fwc = fw
        else:
            q = core - 4
            x = xp[8 * q:8 * q + 8].reshape(NT, D)
            cvec = I["c_ctx"]
            h0 = np.zeros((2, 2048, 128), f)
            pc[:, 0:16] = np.tile(np.array([0.0, 1.0], f), 8)[None, :]
            pc[:, 16:32] = np.tile(np.array([1.0, 0.0], f), 8)[None, :]
            pc[:, 32] = -1.0
            nffv = np.zeros(33, f)
            nffv[0::4] = -1.0
            pc[:, 33:66] = nffv[None, :]
            fwc = fw_prompt
        m = dict(shared)
        m.update(xin=np.ascontiguousarray(x, dtype=f), rows=rows_for(cvec, fwc), rowb=rowb, pc=pc, h0=np.ascontiguousarray(h0, dtype=f))
        maps.append(m)
    return maps


_NC_CACHE = {}


def kernel(**inputs):
    I = {k: np.asarray(v) for k, v in inputs.items()}
    maps = _prep_inputs(I)
    if "nc" not in _NC_CACHE:
        _NC_CACHE["nc"] = build_program()
    nc = _NC_CACHE["nc"]
    res = run_bass_kernel_spmd(nc, maps, core_ids=list(range(8)))
    R = res.results
    y_sample = np.stack([R[b]["yout"] for b in range(4)]).astype(np.float32)
    y_prompt = np.concatenate([R[4 + q]["yout"].reshape(8, 256, D) for q in range(4)], axis=0).astype(np.float32)
    nsf = np.concatenate([R[4 + q]["stf"].reshape(8, 1, 32, 64, 128) for q in range(4)], axis=0).astype(np.float32)
    nsb = np.concatenate([R[4 + q]["stb"].reshape(8, 1, 32, 64, 128) for q in range(4)], axis=0).astype(np.float32)
    return (y_prompt, y_sample, nsf, nsb)
```
